# Optimizing a Trainium2 kernel written in Bass

```python
import jax, jax.numpy as jnp
from jax import lax
import numpy as np

D_MODEL = 2048
BATCH = 2
SEQ = 8192
DEPTH = 4

N_MIXERS = 2
N_NSA_LAYERS = (DEPTH + N_MIXERS - 1) // N_MIXERS
N_HGRN_LAYERS = DEPTH // N_MIXERS
RMS_EPS = 1e-6
NEG = -1e30
BIG = 1e30

NSA_HEADS = 16
NSA_KV_GROUPS = 4
NSA_HEAD_DIM = 128
NSA_REP = NSA_HEADS // NSA_KV_GROUPS
CMP_BLOCK = 32
CMP_STRIDE = 16
CMP_HIDDEN = NSA_HEAD_DIM
SEL_BLOCK = 64
SEL_TOPK = 16
WINDOW = 512
Q_BLOCK = 128
SEL_Q_CHUNK = 16
ROPE_THETA = 500000.0
ROPE_DIM = NSA_HEAD_DIM // 4
NSA_KVW = NSA_KV_GROUPS * NSA_HEAD_DIM
NSA_IN = NSA_HEADS * NSA_HEAD_DIM + 6 * NSA_KVW + 3 * NSA_HEADS + NSA_HEADS * NSA_HEAD_DIM

HGRN_HEADS = 16
HGRN_F_DIM = 128
HGRN_I_DIM = D_MODEL // HGRN_HEADS
HGRN_CHUNK = 32
HGRN_IN = 2 * HGRN_HEADS * HGRN_F_DIM + 2 * HGRN_HEADS * HGRN_I_DIM

kernel_name = "nsa_hgrn2_interleaved_hybrid"


def rmsnorm(x, w):
    xf = x.astype(jnp.float32)
    y = xf * lax.rsqrt(jnp.mean(xf * xf, axis=-1, keepdims=True) + RMS_EPS)
    return (y * w.astype(jnp.float32)).astype(x.dtype)


def partial_rope(t, pos):
    half = ROPE_DIM // 2
    inv = ROPE_THETA ** (-jnp.arange(half, dtype=jnp.float32) / half)
    ang = pos.astype(jnp.float32)[..., None] * inv
    cos = jnp.cos(ang)[:, :, None, :]
    sin = jnp.sin(ang)[:, :, None, :]
    tr = t[..., :ROPE_DIM].astype(jnp.float32)
    t1, t2 = tr[..., :half], tr[..., half:]
    rot = jnp.concatenate([t1 * cos - t2 * sin, t2 * cos + t1 * sin], axis=-1)
    return jnp.concatenate([rot.astype(t.dtype), t[..., ROPE_DIM:]], axis=-1)


def compress_blocks(t, pos_emb, w1, w2):
    S = t.shape[2]
    n_cmp = (S - CMP_BLOCK) // CMP_STRIDE + 1
    idx = jnp.arange(n_cmp)[:, None] * CMP_STRIDE + jnp.arange(CMP_BLOCK)[None, :]
    blocks = t[:, :, idx, :] + pos_emb
    hid = jax.nn.silu(jnp.einsum('bgnld,ldh->bgnh', blocks, w1))
    return jnp.einsum('bgnh,he->bgne', hid, w2)


def nsa_mixer(h, pos, w_in, ck_pos, ck_w1, ck_w2, cv_pos, cv_w1, cv_w2, w_out):
    B, S, _ = h.shape
    H, G, R, d = NSA_HEADS, NSA_KV_GROUPS, NSA_REP, NSA_HEAD_DIM
    scale = d ** -0.5
    proj = h @ w_in
    cuts = [int(c) for c in np.cumsum([H * d] + [NSA_KVW] * 6 + [3 * H])]
    q, kc, vc, ks, vs, kw, vw, gl, z = jnp.split(proj, cuts, axis=-1)

    q = partial_rope(q.reshape(B, S, H, d), pos)
    qg = q.reshape(B, S, G, R, d).transpose(0, 2, 3, 1, 4)

    def kv_layout(t, rotate):
        t = t.reshape(B, S, G, d)
        if rotate:
            t = partial_rope(t, pos)
        return t.transpose(0, 2, 1, 3)

    k_cmp = compress_blocks(kv_layout(kc, True), ck_pos, ck_w1, ck_w2)
    v_cmp = compress_blocks(kv_layout(vc, False), cv_pos, cv_w1, cv_w2)
    k_sel, v_sel = kv_layout(ks, True), kv_layout(vs, False)
    k_win, v_win = kv_layout(kw, True), kv_layout(vw, False)

    n_cmp = k_cmp.shape[2]
    n_sel = S // SEL_BLOCK
    cmp_start = jnp.arange(n_cmp) * CMP_STRIDE
    cmp_end = cmp_start + CMP_BLOCK - 1
    sel_ids = jnp.arange(n_sel)
    overlap = ((cmp_start[:, None] < (sel_ids[None, :] + 1) * SEL_BLOCK)
               & ((cmp_start + CMP_BLOCK)[:, None] > sel_ids[None, :] * SEL_BLOCK)).astype(jnp.float32)

    kw_pad = jnp.pad(k_win, ((0, 0), (0, 0), (WINDOW, 0), (0, 0)))
    vw_pad = jnp.pad(v_win, ((0, 0), (0, 0), (WINDOW, 0), (0, 0)))
    span = WINDOW + Q_BLOCK

    nqb = S // Q_BLOCK
    q_blocks = qg.reshape(B, G, R, nqb, Q_BLOCK, d).transpose(3, 0, 1, 2, 4, 5)
    starts = jnp.arange(nqb, dtype=jnp.int32) * Q_BLOCK

    def cmp_win_step(inp):
        qb, s0 = inp
        t = s0 + jnp.arange(Q_BLOCK)
        sc = jnp.einsum('bgrqd,bgnd->bgrqn', qb, k_cmp).astype(jnp.float32) * scale
        mc = cmp_end[None, :] <= t[:, None]
        pc = jax.nn.softmax(jnp.where(mc, sc, NEG), axis=-1) * mc
        o_c = jnp.einsum('bgrqn,bgnd->bgrqd', pc.astype(v_cmp.dtype), v_cmp)
        imp = jnp.einsum('bgrqn,nk->bgqk', pc, overlap)
        kb = lax.dynamic_slice_in_dim(kw_pad, s0, span, axis=2)
        vb = lax.dynamic_slice_in_dim(vw_pad, s0, span, axis=2)
        kp = s0 - WINDOW + jnp.arange(span)
        mw = (kp[None, :] <= t[:, None]) & (kp[None, :] > t[:, None] - WINDOW) & (kp[None, :] >= 0)
        sw = jnp.einsum('bgrqd,bgkd->bgrqk', qb, kb).astype(jnp.float32) * scale
        pw = jax.nn.softmax(jnp.where(mw, sw, NEG), axis=-1)
        o_w = jnp.einsum('bgrqk,bgkd->bgrqd', pw.astype(vb.dtype), vb)
        return o_c, o_w, imp

    o_c, o_w, imp = lax.map(cmp_win_step, (q_blocks, starts))

    def from_blocks(o):
        return o.transpose(1, 0, 4, 2, 3, 5).reshape(B, S, H, d)

    o_cmp, o_win = from_blocks(o_c), from_blocks(o_w)
    imp = imp.transpose(1, 2, 0, 3, 4).reshape(B, G, S, n_sel)

    tpos = jnp.arange(S)
    cur = tpos // SEL_BLOCK
    valid = sel_ids[None, :] <= cur[:, None]
    forced = (sel_ids[None, :] == 0) | (sel_ids[None, :] == cur[:, None]) | (sel_ids[None, :] == cur[:, None] - 1)
    score = jnp.where(valid, jnp.where(forced, BIG, imp), NEG)
    n_top = min(SEL_TOPK, n_sel)
    _, sel_idx = lax.top_k(score, n_top)

    k_blk = k_sel.reshape(B, G, n_sel, SEL_BLOCK, d)
    v_blk = v_sel.reshape(B, G, n_sel, SEL_BLOCK, d)
    nsc = S // SEL_Q_CHUNK
    q_chunks = qg.reshape(B, G, R, nsc, SEL_Q_CHUNK, d).transpose(3, 0, 1, 2, 4, 5)
    idx_chunks = sel_idx.reshape(B, G, nsc, SEL_Q_CHUNK, n_top).transpose(2, 0, 1, 3, 4)
    c_starts = jnp.arange(nsc, dtype=jnp.int32) * SEL_Q_CHUNK
    gather = jax.vmap(jax.vmap(lambda tb, ib: tb[ib]))

    def sel_step(inp):
        qc, ic, s0 = inp
        t = s0 + jnp.arange(SEL_Q_CHUNK)
        kg = gather(k_blk, ic)
        vg = gather(v_blk, ic)
        keypos = ic[..., None] * SEL_BLOCK + jnp.arange(SEL_BLOCK)
        m = keypos <= t[None, None, :, None, None]
        s = jnp.einsum('bgrqd,bgqnld->bgrqnl', qc, kg).astype(jnp.float32) * scale
        s = jnp.where(m[:, :, None], s, NEG).reshape(B, G, R, SEL_Q_CHUNK, n_top * SEL_BLOCK)
        p = jax.nn.softmax(s, axis=-1).reshape(B, G, R, SEL_Q_CHUNK, n_top, SEL_BLOCK)
        return jnp.einsum('bgrqnl,bgqnld->bgrqd', p.astype(vg.dtype), vg)

    o_s = lax.map(sel_step, (q_chunks, idx_chunks, c_starts))
    o_sel = from_blocks(o_s)

    g = jax.nn.sigmoid(gl.astype(jnp.float32)).reshape(B, S, 3, H)[..., None]
    o = (g[:, :, 0] * o_cmp.astype(jnp.float32) + g[:, :, 1] * o_sel.astype(jnp.float32)
         + g[:, :, 2] * o_win.astype(jnp.float32))
    o = o.reshape(B, S, H * d) * jax.nn.silu(z.astype(jnp.float32))
    return o.astype(h.dtype) @ w_out


def hgrn2_mixer(h, lb, w_in, gnorm_w, w_out):
    B, S, _ = h.shape
    H, Df, Di, C = HGRN_HEADS, HGRN_F_DIM, HGRN_I_DIM, HGRN_CHUNK
    proj = h @ w_in
    q, f, i, z = jnp.split(proj, [H * Df, 2 * H * Df, 2 * H * Df + H * Di], axis=-1)
    fl = f.astype(jnp.float32)
    lb = lb.astype(jnp.float32)
    log_f = jnp.logaddexp(jnp.log(lb), jnp.log1p(-lb) + jax.nn.log_sigmoid(fl))
    k = (1.0 - lb) * jax.nn.sigmoid(-fl)
    q = jax.nn.silu(q.astype(jnp.float32))
    n_chunks = S // C

    def to_chunks(t, dim):
        return t.reshape(B, n_chunks, C, H, dim).transpose(1, 0, 3, 2, 4)

    xs = (to_chunks(q, Df), to_chunks(k, Df), to_chunks(i.astype(jnp.float32), Di), to_chunks(log_f, Df))
    tri = jnp.tril(jnp.ones((C, C), dtype=bool))

    def step(state, inp):
        qc, kc, ic, gc = inp
        b = jnp.cumsum(gc, axis=2)
        diff = b[:, :, :, None, :] - b[:, :, None, :, :]
        decay = jnp.exp(jnp.where(tri[None, None, :, :, None], diff, -jnp.inf))
        att = jnp.einsum('bhtd,bhtsd,bhsd->bhts', qc, decay, kc)
        o = jnp.einsum('bhts,bhsi->bhti', att, ic) + jnp.einsum('bhtd,bhdi->bhti', qc * jnp.exp(b), state)
        b_last = b[:, :, -1:, :]
        new_state = (jnp.exp(b_last[:, :, 0, :])[..., None] * state
                     + jnp.einsum('bhsd,bhsi->bhdi', kc * jnp.exp(b_last - b), ic))
        return new_state, o

    state0 = jnp.zeros((B, H, Df, Di), jnp.float32)
    _, o = lax.scan(step, state0, xs)
    o = o.transpose(1, 0, 3, 2, 4).reshape(B, S, H, Di)
    o = o * lax.rsqrt(jnp.mean(o * o, axis=-1, keepdims=True) + RMS_EPS) * gnorm_w.astype(jnp.float32)
    o = o * jax.nn.silu(z.astype(jnp.float32).reshape(B, S, H, Di))
    return o.reshape(B, S, H * Di).astype(h.dtype) @ w_out


def setup_inputs(seed: int = 0) -> dict:
    key = jax.random.key(seed)
    ks = jax.random.split(key, 20)
    f32 = jnp.float32
    D, d, l = D_MODEL, NSA_HEAD_DIM, CMP_BLOCK
    offset = jax.random.randint(ks[0], (BATCH,), 0, 4096, dtype=jnp.int32)
    positions = offset[:, None] + jnp.arange(SEQ, dtype=jnp.int32)[None, :]
    return {
        "x": jax.random.normal(ks[1], (BATCH, SEQ, D), f32),
        "positions": positions,
        "norm_w": 1.0 + 0.01 * jax.random.normal(ks[2], (DEPTH, D), f32),
        "final_norm_w": 1.0 + 0.01 * jax.random.normal(ks[3], (D,), f32),
        "nsa_w_in": jax.random.normal(ks[4], (N_NSA_LAYERS, D, NSA_IN), f32) * D ** -0.5,
        "nsa_ck_pos": 0.02 * jax.random.normal(ks[5], (N_NSA_LAYERS, l, d), f32),
        "nsa_ck_w1": jax.random.normal(ks[6], (N_NSA_LAYERS, l, d, CMP_HIDDEN), f32) * (l * d) ** -0.5,
        "nsa_ck_w2": jax.random.normal(ks[7], (N_NSA_LAYERS, CMP_HIDDEN, d), f32) * CMP_HIDDEN ** -0.5,
        "nsa_cv_pos": 0.02 * jax.random.normal(ks[8], (N_NSA_LAYERS, l, d), f32),
        "nsa_cv_w1": jax.random.normal(ks[9], (N_NSA_LAYERS, l, d, CMP_HIDDEN), f32) * (l * d) ** -0.5,
        "nsa_cv_w2": jax.random.normal(ks[10], (N_NSA_LAYERS, CMP_HIDDEN, d), f32) * CMP_HIDDEN ** -0.5,
        "nsa_w_out": jax.random.normal(ks[11], (N_NSA_LAYERS, NSA_HEADS * d, D), f32) * (NSA_HEADS * d) ** -0.5,
        "hgrn_w_in": jax.random.normal(ks[12], (N_HGRN_LAYERS, D, HGRN_IN), f32) * D ** -0.5,
        "hgrn_lb_logits": jax.random.normal(ks[13], (N_HGRN_LAYERS, HGRN_HEADS * HGRN_F_DIM), f32),
        "hgrn_gnorm_w": 1.0 + 0.01 * jax.random.normal(ks[14], (N_HGRN_LAYERS, HGRN_I_DIM), f32),
        "hgrn_w_out": jax.random.normal(ks[15], (N_HGRN_LAYERS, HGRN_HEADS * HGRN_I_DIM, D), f32) * (HGRN_HEADS * HGRN_I_DIM) ** -0.5,
    }


def reference(x, positions, norm_w, final_norm_w, nsa_w_in, nsa_ck_pos, nsa_ck_w1, nsa_ck_w2,
              nsa_cv_pos, nsa_cv_w1, nsa_cv_w2, nsa_w_out, hgrn_w_in, hgrn_lb_logits,
              hgrn_gnorm_w, hgrn_w_out):
    p_lb = jax.nn.softmax(hgrn_lb_logits.astype(jnp.float32), axis=0)
    cum = jnp.cumsum(p_lb, axis=0)
    lower_bounds = cum - cum[0]
    for layer in range(DEPTH):
        hn = rmsnorm(x, norm_w[layer])
        j = layer // N_MIXERS
        if layer % N_MIXERS == 0:
            y = nsa_mixer(hn, positions, nsa_w_in[j], nsa_ck_pos[j], nsa_ck_w1[j], nsa_ck_w2[j],
                          nsa_cv_pos[j], nsa_cv_w1[j], nsa_cv_w2[j], nsa_w_out[j])
        else:
            y = hgrn2_mixer(hn, lower_bounds[j], hgrn_w_in[j], hgrn_gnorm_w[j], hgrn_w_out[j])
        x = x + y.astype(x.dtype)
    return rmsnorm(x, final_norm_w)
```

```python
import contextlib
import math
import numpy as np
import ml_dtypes
import concourse.bass as bass
import concourse.mybir as mybir
from concourse.bass_utils import run_bass_kernel_spmd

F32 = mybir.dt.float32
BF16 = mybir.dt.bfloat16
I32 = mybir.dt.int32
AF = mybir.ActivationFunctionType
ALU = mybir.AluOpType
AX = mybir.AxisListType
NPBF = ml_dtypes.bfloat16

D_MODEL = 2048
BATCH = 2
SEQ = 8192
NTOK = BATCH * SEQ
NCORE = 8
TPC = NTOK // NCORE
RMS_EPS = 1e-6
ROPE_THETA = 500000.0

ENGS = ("pe", "act", "dve", "pool", "sp")
SEM_CH = 12000
N_DMA_SEMS = 8


class Op:
    __slots__ = ("eng", "fn", "deps", "needs_inc", "is_dma", "sem", "token", "prev")

    def __init__(self, eng, fn, deps, is_dma):
        self.eng = eng
        self.fn = fn
        self.deps = [d for d in deps if d is not None]
        self.needs_inc = False
        self.is_dma = is_dma
        self.sem = None
        self.token = None
        self.prev = 0


class Prog:
    def __init__(self, nc):
        self.nc = nc
        self.q = {e: [] for e in ENGS}
        self.stack = contextlib.ExitStack()
        self._n = 0

    def sbuf(self, shape, dtype, name=None):
        self._n += 1
        return self.stack.enter_context(self.nc.sbuf_tensor((name + "_sb") if name else f"sb{self._n}", list(shape), dtype))

    def psum(self, shape, dtype=F32, name=None):
        self._n += 1
        return self.stack.enter_context(self.nc.psum_tensor(name or f"ps{self._n}", list(shape), dtype))

    def sem(self, name=None):
        self._n += 1
        return self.stack.enter_context(self.nc.semaphore(name or f"sem{self._n}"))

    def op(self, eng, fn, deps=()):
        o = Op(eng, fn, deps, False)
        for d in o.deps:
            d.needs_inc = True
        self.q[eng].append(o)
        return o

    def dma(self, eng, out, in_, deps=(), **kw):
        o = Op(eng, lambda e: e.dma_start(out=out, in_=in_, **kw), deps, True)
        for d in o.deps:
            d.needs_inc = True
        o.needs_inc = True
        self.q[eng].append(o)
        return o

    def wait_all(self, eng, deps):
        return self.op(eng, None, deps)

    def emit(self):
        nc = self.nc
        eng_sems = {e: [] for e in ENGS}
        dma_sems = {e: [self.sem(f"dq_{e}_{i}") for i in range(N_DMA_SEMS)] for e in ("sp", "act", "pool")}
        dma_cnt = {}
        for e in ENGS:
            cnt = 0
            ndma = 0
            for o in self.q[e]:
                if o.is_dma:
                    s = dma_sems[e][ndma % N_DMA_SEMS]
                    ndma += 1
                    prev = dma_cnt.get(id(s), 0)
                    o.sem = s
                    o.token = prev + 16
                    o.prev = prev
                    dma_cnt[id(s)] = prev + 16
                elif o.needs_inc:
                    k = cnt // SEM_CH
                    while len(eng_sems[e]) <= k:
                        eng_sems[e].append(self.sem(f"pg_{e}_{len(eng_sems[e])}"))
                    o.sem = eng_sems[e][k]
                    o.token = cnt % SEM_CH + 1
                    cnt += 1
        engmap = {"pe": "tensor", "act": "scalar", "dve": "vector", "pool": "gpsimd", "sp": "sync"}
        with nc.Block() as block:
            for e in ENGS:
                ops = self.q[e]
                if not ops:
                    continue

                def body(eng, ops=ops):
                    waited = {}
                    for o in ops:
                        need = {}
                        for d in o.deps:
                            key = id(d.sem)
                            if waited.get(key, 0) >= d.token:
                                continue
                            if key not in need or need[key][1] < d.token:
                                need[key] = (d.sem, d.token)
                        if o.is_dma and o.prev > 0 and waited.get(id(o.sem), 0) < o.prev:
                            key = id(o.sem)
                            if key not in need or need[key][1] < o.prev:
                                need[key] = (o.sem, o.prev)
                        for key, (s, v) in need.items():
                            eng.wait_ge(s, v)
                            waited[key] = v
                        if o.fn is None:
                            continue
                        ins = o.fn(eng)
                        if o.is_dma:
                            ins.then_inc(o.sem, 16)
                        elif o.needs_inc:
                            ins.then_inc(o.sem, 1)

                getattr(block, engmap[e])(body)
        self.stack.close()


class Ring:
    def __init__(self, bufs):
        self.bufs = bufs
        self.i = 0
        self.readers = [[] for _ in bufs]

    def next(self):
        k = self.i % len(self.bufs)
        self.i += 1
        r = self.readers[k]
        self.readers[k] = []
        return k, self.bufs[k], r

    def read_by(self, k, *ops):
        self.readers[k].extend(ops)


def build_proj(nout, blocks, mode, outs, rope, dump_hn=False, nt_limit=None):
    nc = bass.Bass("TRN2", target_bir_lowering=False)
    P = Prog(nc)
    NT = TPC // 128
    KC = D_MODEL // 128
    w = nc.dram_tensor("w", [D_MODEL, nout], F32, kind="ExternalInput").ap()
    wv = w.rearrange("(kc p) n -> p kc n", p=128)
    need_x = (mode == "norm") or any(b[2] == "resid" for b in blocks)
    if need_x:
        x = nc.dram_tensor("x", [TPC, D_MODEL], F32, kind="ExternalInput").ap()
    if mode == "norm":
        nw = nc.dram_tensor("nw", [128, KC], F32, kind="ExternalInput").ap()
        ident_d = nc.dram_tensor("ident", [128, 128], F32, kind="ExternalInput").ap()
    else:
        aT = nc.dram_tensor("aT", [128, KC, TPC], BF16, kind="ExternalInput").ap()
    if rope:
        pos = nc.dram_tensor("pos", [128, NT], I32, kind="ExternalInput").ap()
        invf = nc.dram_tensor("invf", [128, 16], F32, kind="ExternalInput").ap()
    out_aps = {k: nc.dram_tensor(k, [TPC, v[0]], v[1], kind="ExternalOutput").ap() for k, v in outs.items()}

    hnT = P.sbuf([128, KC, TPC], BF16, "hnT")
    out_dmas = []
    if dump_hn:
        hn_out = nc.dram_tensor("hn", [128, KC, TPC], BF16, kind="ExternalOutput").ap()

    if mode == "norm":
        nw_sb = P.sbuf([128, KC], F32, "nw_sb")
        ident = P.sbuf([128, 128], F32, "ident_sb")
        d_nw = P.dma("sp", nw_sb[:], nw)
        d_id = P.dma("sp", ident[:], ident_d)
        ss = P.sbuf([128, NT], F32, "ss")
        rstd = P.sbuf([128, NT], F32, "rstd")
        junk = P.sbuf([128, D_MODEL], BF16, "junk")
        xr = Ring([P.sbuf([128, D_MODEL], F32, f"xt{i}") for i in range(2)])
        xsr = Ring([P.sbuf([128, D_MODEL], F32, f"xs{i}") for i in range(2)])
        tpr = Ring([P.psum([128, 512], F32, f"tp{i}") for i in range(2)])
        junk_last = None
        for t in range(NT if nt_limit is None else nt_limit):
            k, xt, rd = xr.next()
            d_x = P.dma("sp", xt[:], x[t * 128:(t + 1) * 128, :], deps=rd)
            a_sq = P.op("act", lambda e, xt=xt, t=t: e.activation(out=junk[:], in_=xt[:], func=AF.Square,
                                                                 accum_out=ss[:, t:t + 1]), deps=[d_x])
            junk_last = a_sq
            v1 = P.op("dve", lambda e, t=t: e.tensor_scalar(out=rstd[:, t:t + 1], in0=ss[:, t:t + 1],
                                                            scalar1=1.0 / D_MODEL, scalar2=RMS_EPS,
                                                            op0=ALU.mult, op1=ALU.add), deps=[a_sq])
            a_rt = P.op("act", lambda e, t=t: e.activation(out=rstd[:, t:t + 1], in_=rstd[:, t:t + 1], func=AF.Sqrt),
                        deps=[v1])
            v2 = P.op("dve", lambda e, t=t: e.reciprocal(out=rstd[:, t:t + 1], in_=rstd[:, t:t + 1]), deps=[a_rt])
            ks, xs, rds = xsr.next()
            v3 = P.op("dve", lambda e, xs=xs, xt=xt, t=t: e.tensor_scalar(out=xs[:], in0=xt[:], scalar1=rstd[:, t:t + 1],
                                                                          scalar2=None, op0=ALU.mult),
                      deps=[v2, d_x] + rds)
            xr.read_by(k, a_sq, v3)
            for g in range(KC // 4):
                kp, tp, rdp = tpr.next()
                last = None
                for j in range(4):
                    kc = g * 4 + j
                    last = P.op("pe", lambda e, tp=tp, xs=xs, kc=kc, j=j: e.transpose(
                        out=tp[:, j * 128:(j + 1) * 128], in_=xs[:, kc * 128:(kc + 1) * 128], identity=ident[:]),
                        deps=[v3, d_id] + rdp)
                evs = []
                for j in range(4):
                    kc = g * 4 + j
                    eng = "dve" if kp == 0 else "act"
                    if eng == "dve":
                        ev = P.op("dve", lambda e, tp=tp, kc=kc, j=j, t=t: e.tensor_scalar(
                            out=hnT[:, kc, t * 128:(t + 1) * 128], in0=tp[:, j * 128:(j + 1) * 128],
                            scalar1=nw_sb[:, kc:kc + 1], scalar2=None, op0=ALU.mult), deps=[last, d_nw])
                    else:
                        ev = P.op("act", lambda e, tp=tp, kc=kc, j=j, t=t: e.activation(
                            out=hnT[:, kc, t * 128:(t + 1) * 128], in_=tp[:, j * 128:(j + 1) * 128],
                            func=AF.Copy, scale=nw_sb[:, kc:kc + 1]), deps=[last, d_nw])
                    evs.append(ev)
                tpr.read_by(kp, *evs)
                if g == KC // 4 - 1:
                    xsr.read_by(ks, last)
            hn_ready_t = evs
        hn_ready = [P.q["dve"][-1], P.q["act"][-1]]
        for o in hn_ready:
            o.needs_inc = True
    else:
        hn_ready = []
        for kc0 in range(0, KC, 4):
            hn_ready.append(P.dma("sp", hnT[:, kc0:kc0 + 4, :], aT[:, kc0:kc0 + 4, :]))

    if dump_hn:
        out_dmas.append(P.dma("sp", hn_out, hnT[:], deps=hn_ready))
    if rope:
        posi = P.sbuf([128, NT], I32, "posi")
        posf = P.sbuf([128, NT], F32, "posf")
        invf_sb = P.sbuf([128, 16], F32, "invf_sb")
        ang = P.sbuf([128, NT, 16], F32, "ang")
        kq = P.sbuf([128, NT, 16], F32, "kq")
        kqi = P.sbuf([128, NT, 16], I32, "kqi")
        red = P.sbuf([128, NT, 16], F32, "red")
        sarg = P.sbuf([128, NT, 16], F32, "sarg")
        carg = P.sbuf([128, NT, 16], F32, "carg")
        sn = P.sbuf([128, NT, 16], F32, "sn")
        cs = P.sbuf([128, NT, 16], F32, "cs")
        d_p = P.dma("sp", posi[:], pos)
        d_f = P.dma("sp", invf_sb[:], invf)
        r0 = P.op("dve", lambda e: e.tensor_copy(out=posf[:], in_=posi[:]), deps=[d_p])
        r1 = P.op("dve", lambda e: e.tensor_tensor(out=ang[:], in0=posf[:].unsqueeze(2).to_broadcast([128, NT, 16]),
                                                   in1=invf_sb[:].unsqueeze(1).to_broadcast([128, NT, 16]),
                                                   op=ALU.mult), deps=[r0, d_f])
        r2 = P.op("dve", lambda e: e.tensor_scalar(out=kq[:], in0=ang[:], scalar1=1.0 / (2 * math.pi), scalar2=None,
                                                   op0=ALU.mult), deps=[r1])
        r3 = P.op("dve", lambda e: e.tensor_copy(out=kqi[:], in_=kq[:]), deps=[r2])
        r4 = P.op("dve", lambda e: e.tensor_copy(out=kq[:], in_=kqi[:]), deps=[r3])
        c1 = 6.28125
        c2 = float(np.float32(2 * math.pi - 6.28125))
        c3 = float(2 * math.pi - 6.28125 - c2)
        TWO_PI = 2 * math.pi
        r5a = P.op("dve", lambda e: e.scalar_tensor_tensor(out=red[:], in0=kq[:], scalar=-c1, in1=ang[:], op0=ALU.mult, op1=ALU.add), deps=[r4])
        r5b = P.op("dve", lambda e: e.scalar_tensor_tensor(out=red[:], in0=kq[:], scalar=-c2, in1=red[:], op0=ALU.mult, op1=ALU.add), deps=[r5a])
        r5 = P.op("dve", lambda e: e.scalar_tensor_tensor(out=red[:], in0=kq[:], scalar=-c3, in1=red[:], op0=ALU.mult, op1=ALU.add), deps=[r5b])
        w1 = P.op("dve", lambda e: e.tensor_scalar(out=carg[:], in0=red[:], scalar1=math.pi, scalar2=-TWO_PI, op0=ALU.is_gt, op1=ALU.mult), deps=[r5])
        w2 = P.op("dve", lambda e: e.tensor_tensor(out=red[:], in0=red[:], in1=carg[:], op=ALU.add), deps=[w1])
        w3 = P.op("dve", lambda e: e.tensor_scalar(out=carg[:], in0=red[:], scalar1=-math.pi, scalar2=TWO_PI, op0=ALU.is_lt, op1=ALU.mult), deps=[w2])
        r6 = P.op("dve", lambda e: e.tensor_tensor(out=sarg[:], in0=red[:], in1=carg[:], op=ALU.add), deps=[w3])
        w4 = P.op("dve", lambda e: e.tensor_scalar(out=red[:], in0=sarg[:], scalar1=math.pi / 2, scalar2=None, op0=ALU.add), deps=[r6])
        w5 = P.op("dve", lambda e: e.tensor_scalar(out=kq[:], in0=red[:], scalar1=math.pi, scalar2=-TWO_PI, op0=ALU.is_gt, op1=ALU.mult), deps=[w4])
        r7 = P.op("dve", lambda e: e.tensor_tensor(out=carg[:], in0=red[:], in1=kq[:], op=ALU.add), deps=[w5])
        lim = 3.14159
        r8 = P.op("dve", lambda e: e.tensor_scalar(out=sarg[:], in0=sarg[:], scalar1=lim, scalar2=-lim,
                                                   op0=ALU.min, op1=ALU.max), deps=[r6])
        r9 = P.op("dve", lambda e: e.tensor_scalar(out=carg[:], in0=carg[:], scalar1=lim, scalar2=-lim,
                                                   op0=ALU.min, op1=ALU.max), deps=[r7])
        rs = P.op("act", lambda e: e.activation(out=sn[:], in_=sarg[:], func=AF.Sin), deps=[r8])
        rc = P.op("act", lambda e: e.activation(out=cs[:], in_=carg[:], func=AF.Sin), deps=[r9])
        rope_ready = [rs, rc]
        ra = P.sbuf([128, 4, 16], F32, "ra")
        rb = P.sbuf([128, 4, 16], F32, "rb")

    NB = 512
    wr = Ring([P.sbuf([128, KC, NB], BF16, f"wb{i}") for i in range(2)])
    accr = Ring([P.psum([128, NB], F32, f"acc{i}") for i in range(4)])
    st32 = Ring([P.sbuf([128, NB], F32, f"st32_{i}") for i in range(3)])
    st16 = Ring([P.sbuf([128, NB], BF16, f"st16_{i}") for i in range(3)])
    if any(b[2] == "resid" for b in blocks):
        xres = Ring([P.sbuf([128, NB], F32, f"xres{i}") for i in range(3)])
    rope_last = None
    for (c0, n, kind, oname, oc0) in blocks:
        kw_, wb, rdw = wr.next()
        wl = []
        for kc0 in range(0, KC, 4):
            wl.append(P.dma("pool", wb[:, kc0:kc0 + 4, 0:n], wv[:, kc0:kc0 + 4, c0:c0 + n], deps=rdw))
        for t in range(NT):
            ka, acc, rda = accr.next()
            mm = None
            for kc in range(KC):
                mm = P.op("pe", lambda e, acc=acc, wb=wb, kc=kc, t=t, n=n: e.matmul(
                    acc[:, 0:n], lhsT=hnT[:, kc, t * 128:(t + 1) * 128], rhs=wb[:, kc, 0:n],
                    start=(kc == 0), stop=(kc == KC - 1)),
                    deps=(hn_ready + wl + rda) if kc == 0 else ())
            wr.read_by(kw_, mm)
            oap = out_aps[oname][t * 128:(t + 1) * 128, oc0:oc0 + n]
            if kind == "bf16":
                k2, sb, rd2 = st16.next()
                ev = P.op("act", lambda e, sb=sb, acc=acc, n=n: e.activation(out=sb[:, 0:n], in_=acc[:, 0:n], func=AF.Copy),
                          deps=[mm] + rd2)
                accr.read_by(ka, ev)
                od = P.dma("sp", oap, sb[:, 0:n], deps=[ev])
                st16.read_by(k2, od)
            elif kind in ("f32", "silu", "sigmoid"):
                k2, sb, rd2 = st32.next()
                fn = {"f32": AF.Copy, "silu": AF.Silu, "sigmoid": AF.Sigmoid}[kind]
                ev = P.op("act", lambda e, sb=sb, acc=acc, n=n, fn=fn: e.activation(out=sb[:, 0:n], in_=acc[:, 0:n], func=fn),
                          deps=[mm] + rd2)
                accr.read_by(ka, ev)
                od = P.dma("sp", oap, sb[:, 0:n], deps=[ev])
                st32.read_by(k2, od)
            elif kind == "resid":
                k3, xb, rd3 = xres.next()
                dx = P.dma("act", xb[:, 0:n], x[t * 128:(t + 1) * 128, c0:c0 + n], deps=rd3)
                k2, sb, rd2 = st32.next()
                ev = P.op("dve", lambda e, sb=sb, acc=acc, xb=xb, n=n: e.tensor_tensor(
                    out=sb[:, 0:n], in0=acc[:, 0:n], in1=xb[:, 0:n], op=ALU.add), deps=[mm, dx] + rd2)
                accr.read_by(ka, ev)
                xres.read_by(k3, ev)
                od = P.dma("sp", oap, sb[:, 0:n], deps=[ev])
                st32.read_by(k2, od)
            elif kind == "rope":
                assert n % 128 == 0
                nh = n // 128
                k2, s32, rd2 = st32.next()
                ev = P.op("act", lambda e, s32=s32, acc=acc, n=n: e.activation(out=s32[:, 0:n], in_=acc[:, 0:n], func=AF.Copy),
                          deps=[mm] + rd2)
                accr.read_by(ka, ev)
                k3, sb, rd3 = st16.next()
                cp = P.op("dve", lambda e, sb=sb, s32=s32, n=n: e.tensor_copy(out=sb[:, 0:n], in_=s32[:, 0:n]),
                          deps=[ev] + rd3)
                s3 = s32[:, 0:n].rearrange("p (h d) -> p h d", d=128)
                o3 = sb[:, 0:n].rearrange("p (h d) -> p h d", d=128)
                t1 = s3[:, :, 0:16]
                t2 = s3[:, :, 16:32]
                cb = cs[:, t, :].unsqueeze(1).to_broadcast([128, nh, 16])
                sbb = sn[:, t, :].unsqueeze(1).to_broadcast([128, nh, 16])
                A = ra[:, 0:nh, :]
                B = rb[:, 0:nh, :]
                q1 = P.op("dve", lambda e, A=A, t1=t1, cb=cb: e.tensor_tensor(out=A, in0=t1, in1=cb, op=ALU.mult),
                          deps=[ev, rope_last] + rope_ready)
                q2 = P.op("dve", lambda e, B=B, t2=t2, sbb=sbb: e.tensor_tensor(out=B, in0=t2, in1=sbb, op=ALU.mult), deps=[q1])
                q3 = P.op("dve", lambda e, A=A, B=B, o3=o3: e.tensor_tensor(out=o3[:, :, 0:16], in0=A, in1=B, op=ALU.subtract),
                          deps=[q2, cp])
                q4 = P.op("dve", lambda e, A=A, t2=t2, cb=cb: e.tensor_tensor(out=A, in0=t2, in1=cb, op=ALU.mult), deps=[q3])
                q5 = P.op("dve", lambda e, B=B, t1=t1, sbb=sbb: e.tensor_tensor(out=B, in0=t1, in1=sbb, op=ALU.mult), deps=[q4])
                q6 = P.op("dve", lambda e, A=A, B=B, o3=o3: e.tensor_tensor(out=o3[:, :, 16:32], in0=A, in1=B, op=ALU.add),
                          deps=[q5])
                rope_last = q6
                st32.read_by(k2, q6, cp)
                od = P.dma("sp", oap, sb[:, 0:n], deps=[q6])
                st16.read_by(k3, od)
            else:
                raise ValueError(kind)
            out_dmas.append(od)
    P.wait_all("sp", out_dmas)
    P.emit()
    return nc


NSA_SCALE = 128 ** -0.5
BIGV = 1e30


def build_nsa_core(nqb=64):
    nc = bass.Bass("TRN2", target_bir_lowering=False)
    P = Prog(nc)
    S = SEQ

    def din(name, shape, dt):
        return nc.dram_tensor(name, list(shape), dt, kind="ExternalInput").ap()

    qT_d = din("qT", [128, 4, S], BF16)
    kcT_d = din("kcT", [128, S], BF16)
    vcT_d = din("vcT", [128, S], BF16)
    ksT_d = din("ksT", [128, S], BF16)
    kwT_d = din("kwT", [128, S], BF16)
    vs_d = din("vs", [128, 64, 128], BF16)
    vw_d = din("vw", [128, 64, 128], BF16)
    gate_d = din("gate", [128, 64, 12], F32)
    sz_d = din("sz", [128, 64, 512], F32)
    w1k_d = din("w1k", [128, 32, 128], F32)
    w1v_d = din("w1v", [128, 32, 128], F32)
    posk_d = din("posk", [128, 32], F32)
    posv_d = din("posv", [128, 32], F32)
    w2k_d = din("w2k", [128, 128], F32)
    w2v_d = din("w2v", [128, 128], F32)
    E_d = din("E", [128, S], BF16)
    ov_d = din("ov", [128, 4, 128], BF16)
    idb_d = din("identb", [128, 128], BF16)
    o_d = nc.dram_tensor("o", [S, 512], BF16, kind="ExternalOutput").ap()

    ksT = P.sbuf([128, S], BF16, "ksT")
    kwT = P.sbuf([128, S], BF16, "kwT")
    kcT = P.sbuf([128, S], BF16, "kcT")
    vcT = P.sbuf([128, S], BF16, "vcT")
    Emat = P.sbuf([128, S], BF16, "Emat")
    vs_e = P.sbuf([128, 64, 130], BF16, "vs_e")
    vw_e = P.sbuf([128, 64, 130], BF16, "vw_e")
    gates = P.sbuf([128, 64, 12], F32, "gates")
    w1k = P.sbuf([128, 32, 128], BF16, "w1k")
    w1v = P.sbuf([128, 32, 128], BF16, "w1v")
    posk = P.sbuf([128, 32], BF16, "posk")
    posv = P.sbuf([128, 32], BF16, "posv")
    w2k = P.sbuf([128, 128], BF16, "w2k")
    w2v = P.sbuf([128, 128], BF16, "w2v")
    identb = P.sbuf([128, 128], BF16, "identb")
    CV = P.sbuf([128, 4, 258], BF16, "CV")
    kcmpT = P.sbuf([128, 512], BF16, "kcmpT")
    hid_k = P.sbuf([128, 512], BF16, "hid_k")
    hid_v = P.sbuf([128, 512], BF16, "hid_v")
    cb = P.sbuf([128, 2], F32, "cb")

    banks = [P.psum([128, 512], F32, f"bank{i}") for i in range(7)]
    tbank = P.psum([128, 1024], BF16, "tbank")
    S_banks = banks[0:2]
    accA, accB, impb = banks[2], banks[3], banks[4]
    M_banks = banks[5:7]

    ld = {}
    ld["kcT"] = P.dma("sp", kcT[:], kcT_d)
    ld["vcT"] = P.dma("sp", vcT[:], vcT_d)
    for nm, sb, dd in (("w1k", w1k, w1k_d), ("w1v", w1v, w1v_d)):
        ld[nm] = [P.dma("pool", sb[:, l0:l0 + 8, :], dd[:, l0:l0 + 8, :]) for l0 in range(0, 32, 8)]
    ld["posk"] = P.dma("pool", posk[:], posk_d)
    ld["posv"] = P.dma("pool", posv[:], posv_d)
    ld["w2k"] = P.dma("pool", w2k[:], w2k_d)
    ld["w2v"] = P.dma("pool", w2v[:], w2v_d)
    ld["ksT"] = P.dma("act", ksT[:], ksT_d)
    ld["kwT"] = P.dma("act", kwT[:], kwT_d)
    ld["E"] = P.dma("act", Emat[:], E_d)
    ld["vs"] = P.dma("sp", vs_e[:, :, 0:128], vs_d)
    ld["vw"] = P.dma("sp", vw_e[:, :, 0:128], vw_d)
    ld["gates"] = P.dma("sp", gates[:], gate_d)
    ld["ov"] = P.dma("sp", CV[:, :, 0:128], ov_d)
    ld["idb"] = P.dma("sp", identb[:], idb_d)
    ms = [P.op("pool", lambda e: e.memset(hid_k[:, 511:512], 0.0)),
          P.op("pool", lambda e: e.memset(hid_v[:, 511:512], 0.0)),
          P.op("pool", lambda e: e.memset(kcmpT[:, 511:512], 0.0)),
          P.op("pool", lambda e: e.memset(vs_e[:, :, 128:130], 1.0), deps=[ld["vs"]]),
          P.op("pool", lambda e: e.memset(vw_e[:, :, 128:130], 1.0), deps=[ld["vw"]])]
    ones_ready = ms[-1]

    def compress(xT, w1, pos, hid, col, d_x, d_w1, d_pos):
        hp = banks[0][:, 0:511]
        mm = None
        for l in range(32):
            mm = P.op("pe", lambda e, l=l: e.matmul(hp, lhsT=w1[:, l, :], rhs=xT[:, l:l + 8161:16],
                                                    start=(l == 0), stop=(l == 31)), deps=[d_x] + d_w1)
        cp = banks[1][:, 0:1]
        mc = None
        for l in range(32):
            mc = P.op("pe", lambda e, l=l: e.matmul(cp, lhsT=w1[:, l, :], rhs=pos[:, l:l + 1],
                                                    start=(l == 0), stop=(l == 31)), deps=[d_pos] + d_w1)
        c1 = P.op("dve", lambda e: e.tensor_copy(out=cb[:, col:col + 1], in_=cp), deps=[mc])
        h = P.op("act", lambda e: e.activation(out=hid[:, 0:511], in_=hp, func=AF.Silu, bias=cb[:, col:col + 1]),
                 deps=[mm, c1])
        return h, c1

    hk, c1k = compress(kcT, w1k, posk, hid_k, 0, ld["kcT"], ld["w1k"], ld["posk"])
    kp = banks[2][:, 0:511]
    mk = P.op("pe", lambda e: e.matmul(kp, lhsT=w2k[:], rhs=hid_k[:, 0:511], start=True, stop=True), deps=[hk, ld["w2k"]])
    kev = P.op("dve", lambda e: e.tensor_copy(out=kcmpT[:, 0:511], in_=kp), deps=[mk])
    hv_dep = [hk, c1k]
    hp_guard = P.op("pe", None, hv_dep)
    hv, c1v = compress(vcT, w1v, posv, hid_v, 1, ld["vcT"], ld["w1v"], ld["posv"])
    vev = []
    for j in range(4):
        vp = banks[3 + (j % 2)][:, 0:128]
        mv = P.op("pe", lambda e, vp=vp, j=j: e.matmul(vp, lhsT=hid_v[:, j * 128:(j + 1) * 128], rhs=w2v[:],
                                                       start=True, stop=True), deps=[hv, ld["w2v"]] + vev[-2:-1])
        vev.append(P.op("dve", lambda e, vp=vp, j=j: e.tensor_copy(out=CV[:, j, 128:256], in_=vp), deps=[mv]))
    cv_ones = P.op("dve", lambda e: e.memset(CV[:, :, 256:258], 1.0), deps=[vev[-1], vev[-2], ld["ov"]])
    cmp_ready = [kev, vev[-1], vev[-2], ld["ov"], ones_ready, cv_ones]

    qbr = Ring([P.sbuf([128, 4, 128], BF16, f"qb{i}") for i in range(3)])
    Pr = Ring([P.sbuf([128, 4, 128], BF16, f"Pt{i}") for i in range(3)])
    Sr = Ring(S_banks)
    Mr = Ring(M_banks)
    szr = Ring([P.sbuf([128, 512], F32, f"szb{i}") for i in range(2)])
    omr = Ring([P.sbuf([128, 4, 128], F32, f"om{i}") for i in range(2)])
    outr = Ring([P.sbuf([128, 512], BF16, f"o16_{i}") for i in range(2)])
    selTr = Ring([P.sbuf([128, 128], BF16, f"selT{i}") for i in range(2)])
    sc = P.sbuf([128, 128], F32, "score")
    sc2 = P.sbuf([128, 128], F32, "score2")
    m8a = P.sbuf([128, 8], F32, "m8a")
    m8b = P.sbuf([128, 8], F32, "m8b")
    selm = P.sbuf([128, 128], BF16, "selm")
    lt = P.sbuf([128, 4], F32, "lt")
    coef = P.sbuf([128, 4], F32, "coef")

    state = {"acc_readers": list(cmp_ready), "out_dmas": [], "dve_last": None}

    def acc_view(r, n):
        if r < 3:
            return accA[:, r * 129:r * 129 + n]
        return accB[:, 0:n]

    def run_tiles(qb, d_q, tiles, KT, d_K, Vt, d_V, kind, selT_ap=None, selT_ready=None):
        last_pv = None
        nt = len(tiles)
        for ti, (j, mask) in enumerate(tiles):
            ks_, Sb, rdS = Sr.next()
            mmS = P.op("pe", lambda e, Sb=Sb, j=j: e.matmul(Sb[:], lhsT=KT[:, j * 128:(j + 1) * 128],
                                                             rhs=qb[:].rearrange("p r q -> p (r q)"),
                                                             start=True, stop=True), deps=[d_q] + d_K + rdS)
            kp_, Pt, rdP = Pr.next()
            Pf = Pt[:].rearrange("p r q -> p (r q)")
            ex = P.op("act", lambda e, Pf=Pf, Sb=Sb: e.activation(out=Pf, in_=Sb[:], func=AF.Exp, scale=NSA_SCALE),
                      deps=[mmS] + rdP)
            Sr.read_by(ks_, ex)
            lastm = ex
            if kind == "sel":
                km_, Mb, rdM = Mr.next()
                mmM = P.op("pe", lambda e, Mb=Mb, j=j: e.matmul(Mb[:, 0:128], lhsT=Emat[:, j * 128:(j + 1) * 128],
                                                                 rhs=selT_ap, start=True, stop=True),
                           deps=[ld["E"], selT_ready] + rdM)
                mu = P.op("dve", lambda e, Pt=Pt, Mb=Mb: e.tensor_tensor(
                    out=Pt[:], in0=Pt[:], in1=Mb[:, 0:128].unsqueeze(1).to_broadcast([128, 4, 128]), op=ALU.mult),
                    deps=[ex, mmM])
                Mr.read_by(km_, mu)
                lastm = mu
            if mask is not None:
                base, cm, qstep, cop = mask
                lastm = P.op("pool", lambda e, Pt=Pt, base=base, cm=cm, qstep=qstep, cop=cop: e.affine_select(
                    out=Pt[:], in_=Pt[:], pattern=[[0, 4], [qstep, 128]], compare_op=cop, fill=0.0,
                    base=base, channel_multiplier=cm), deps=[lastm])
            first = (ti == 0)
            lastt = (ti == nt - 1)
            for r in range(4):
                n = 129
                rhs = (Vt[:, j, 128:257] if kind == "cmp" else Vt[:, j, 0:129])
                last_pv = P.op("pe", lambda e, r=r, Pt=Pt, rhs=rhs, first=first, lastt=lastt: e.matmul(
                    acc_view(r, 129), lhsT=Pt[:, r, :], rhs=rhs, start=(first and r in (0, 3)), stop=lastt),
                    deps=[lastm] + d_V + (state["acc_readers"] if first else []))
            if kind == "cmp":
                for r in range(4):
                    last_pv = P.op("pe", lambda e, r=r, Pt=Pt, j=j, first=first, lastt=lastt: e.matmul(
                        impb[:, r * 128:(r + 1) * 128], lhsT=Pt[:, r, :], rhs=Vt[:, j, 0:128], start=(first and r == 0), stop=lastt),
                        deps=[lastm])
            Pr.read_by(kp_, last_pv)
        return last_pv

    def evac_branch(last_pv, om, br, i, first_branch, extra_dep):
        ops = []
        d0 = P.op("dve", lambda e: e.tensor_copy(out=lt[:, 0:3], in_=accA[:, 128:387:129]), deps=[last_pv, state["dve_last"]] + extra_dep)
        d1 = P.op("dve", lambda e: e.tensor_copy(out=lt[:, 3:4], in_=accB[:, 128:129]), deps=[last_pv])
        d2 = P.op("dve", lambda e: e.tensor_scalar(out=lt[:], in0=lt[:], scalar1=1e-30, scalar2=None, op0=ALU.max), deps=[d0, d1])
        d3 = P.op("dve", lambda e: e.reciprocal(out=lt[:], in_=lt[:]), deps=[d2])
        d4 = P.op("dve", lambda e: e.tensor_tensor(out=coef[:], in0=lt[:], in1=gates[:, i, br * 4:(br + 1) * 4], op=ALU.mult),
                  deps=[d3, ld["gates"]])
        lastd = d4
        for r in range(4):
            if first_branch:
                lastd = P.op("dve", lambda e, r=r: e.tensor_scalar(out=om[:, r, :], in0=acc_view(r, 128), scalar1=coef[:, r:r + 1],
                                                                    scalar2=None, op0=ALU.mult), deps=[d4])
            else:
                lastd = P.op("dve", lambda e, r=r: e.scalar_tensor_tensor(out=om[:, r, :], in0=acc_view(r, 128),
                                                                           scalar=coef[:, r:r + 1], in1=om[:, r, :],
                                                                           op0=ALU.mult, op1=ALU.add), deps=[d4, lastd])
            ops.append(lastd)
        state["dve_last"] = lastd
        return ops, d3

    for i in range(nqb):
        s0 = 128 * i
        kq_, qb, rdq = qbr.next()
        d_q = P.dma("sp", qb[:], qT_d[:, :, s0:s0 + 128], deps=rdq)
        kz_, szb, rdz = szr.next()
        d_sz = P.dma("sp", szb[:], sz_d[:, i, :], deps=rdz)
        ko_, om, rdo = omr.next()

        tiles = []
        for j in range(4):
            base = s0 - 2048 * j - 31
            if base + 127 < 0:
                continue
            mask = None if base - 16 * 127 >= 0 else (base, -16, 1, ALU.is_ge)
            tiles.append((j, mask))
        lp = run_tiles(qb, d_q, tiles, kcmpT, cmp_ready, CV, cmp_ready, "cmp")
        ev_ops, d3 = evac_branch(lp, om, 0, i, True, rdo)
        lastd = None
        for r in range(4):
            if r == 0:
                lastd = P.op("dve", lambda e: e.tensor_scalar(out=sc[:], in0=impb[:, 0:128], scalar1=lt[:, 0:1], scalar2=None,
                                                              op0=ALU.mult), deps=[d3, lp])
            else:
                lastd = P.op("dve", lambda e, r=r: e.scalar_tensor_tensor(out=sc[:], in0=impb[:, r * 128:(r + 1) * 128],
                                                                           scalar=lt[:, r:r + 1], in1=sc[:], op0=ALU.mult,
                                                                           op1=ALU.add), deps=[lastd])
        imp_done = lastd
        sets = [(slice(0, 128), slice(0, 1), BIGV), (slice(0, 128), slice(2 * i, 2 * i + 1), BIGV),
                (slice(64, 128), slice(2 * i + 1, 2 * i + 2), BIGV), (slice(0, 64), slice(2 * i + 1, 128), -BIGV)]
        if i >= 1:
            sets.append((slice(0, 64), slice(2 * i - 1, 2 * i), BIGV))
        if 2 * i + 2 < 128:
            sets.append((slice(64, 128), slice(2 * i + 2, 128), -BIGV))
        for (ps_, cs_, val) in sets:
            lastd = P.op("dve", lambda e, ps_=ps_, cs_=cs_, val=val: e.memset(sc[ps_, cs_], val), deps=[lastd])
        t1 = P.op("dve", lambda e: e.max(out=m8a[:], in_=sc[:]), deps=[lastd])
        t2 = P.op("dve", lambda e: e.match_replace(out=sc2[:], in_to_replace=m8a[:], in_values=sc[:], imm_value=-3e30), deps=[t1])
        t3 = P.op("dve", lambda e: e.max(out=m8b[:], in_=sc2[:]), deps=[t2])
        t4 = P.op("dve", lambda e: e.tensor_scalar(out=selm[:], in0=sc[:], scalar1=m8b[:, 7:8], scalar2=None, op0=ALU.is_ge),
                  deps=[t3, state.get("selm_reader")])
        state["dve_last"] = t4
        state["acc_readers"] = ev_ops + [imp_done]

        tiles = []
        for j in range(max(0, i - 4), i + 1):
            if j == i - 4:
                mask = (0, 1, -1, ALU.is_gt)
            elif j == i:
                mask = (0, -1, 1, ALU.is_ge)
            else:
                mask = None
            tiles.append((j, mask))
        lp = run_tiles(qb, d_q, tiles, kwT, [ld["kwT"]], vw_e, [ld["vw"], ones_ready], "win")
        ev_ops, _ = evac_branch(lp, om, 2, i, False, [])
        state["acc_readers"] = ev_ops

        tr = P.op("pe", lambda e: e.transpose(out=tbank[:, 0:128], in_=selm[:], identity=identb[:]),
                  deps=[t4, ld["idb"], state.get("tbank_reader")])
        state["selm_reader"] = tr
        kt_, selT, rdt = selTr.next()
        cpT = P.op("dve", lambda e, selT=selT: e.tensor_copy(out=selT[:], in_=tbank[:, 0:128]), deps=[tr] + rdt)
        state["tbank_reader"] = cpT
        state["dve_last"] = cpT

        tiles = [(j, (0, -1, 1, ALU.is_ge) if j == i else None) for j in range(0, i + 1)]
        lp = run_tiles(qb, d_q, tiles, ksT, [ld["ksT"]], vs_e, [ld["vs"], ones_ready], "sel", selT_ap=selT[:], selT_ready=cpT)
        selTr.read_by(kt_, lp)
        qbr.read_by(kq_, lp)
        ev_ops, _ = evac_branch(lp, om, 1, i, False, [])
        state["acc_readers"] = ev_ops

        kout_, o16, rdout = outr.next()
        fin = P.op("dve", lambda e, o16=o16, om=om, szb=szb: e.tensor_tensor(
            out=o16[:], in0=om[:].rearrange("p r d -> p (r d)"), in1=szb[:], op=ALU.mult), deps=ev_ops + [d_sz] + rdout)
        state["dve_last"] = fin
        szr.read_by(kz_, fin)
        omr.read_by(ko_, fin)
        od = P.dma("act", o_d[s0:s0 + 128, :], o16[:], deps=[fin])
        outr.read_by(kout_, od)
        state["out_dmas"].append(od)
    P.wait_all("act", state["out_dmas"])
    P.emit()
    return nc


def build_hgrn_core(layer_j, nst=16):
    nc = bass.Bass("TRN2", target_bir_lowering=False)
    P = Prog(nc)
    S = SEQ

    def din(name, shape, dt):
        return nc.dram_tensor(name, list(shape), dt, kind="ExternalInput").ap()

    qT_d = din("qT", [128, 4, S], F32)
    fT_d = din("fT", [128, 4, S], F32)
    szT_d = din("szT", [128, 4, S], F32)
    iv_d = din("iv", [128, 64, 4, 128], BF16)
    lbl_d = din("lbl", [128, 4, 2], F32)
    gw_d = din("gw", [128, 1], F32)
    mbd_d = din("maskbd", [128, 128], F32)
    cm_d = din("cm", [128, 4], F32)
    seg_d = din("seg", [128, 2048], F32)
    ones_d = din("ones", [128, 128], F32)
    idb_d = din("identb", [128, 128], BF16)
    oT_d = nc.dram_tensor("oT", [128, 4, S], BF16, kind="ExternalOutput").ap()

    lbl = P.sbuf([128, 4, 2], F32, "lbl")
    gw = P.sbuf([128, 1], F32, "gw")
    mbd = P.sbuf([128, 128], F32, "mbd")
    cm = P.sbuf([128, 4], F32, "cm")
    seg = P.sbuf([128, 2048], F32, "seg")
    ones = P.sbuf([128, 128], F32, "ones")
    identb = P.sbuf([128, 128], BF16, "identb")
    lb = P.sbuf([128, 4], F32, "lb")
    oml = P.sbuf([128, 4], F32, "oml")
    ld = {}
    for nm, sb, dd in (("lbl", lbl, lbl_d), ("gw", gw, gw_d), ("mbd", mbd, mbd_d), ("cm", cm, cm_d), ("seg", seg, seg_d),
                       ("ones", ones, ones_d), ("idb", identb, idb_d)):
        ld[nm] = P.dma("act", sb[:], dd)
    if layer_j >= 1:
        assert layer_j == 1
        a0 = P.op("dve", lambda e: e.tensor_tensor(out=lb[:], in0=lbl[:, :, 0], in1=lbl[:, :, 1], op=ALU.subtract), deps=[ld["lbl"]])
        a1 = P.op("act", lambda e: e.activation(out=lb[:], in_=lb[:], func=AF.Exp), deps=[a0])
        a2 = P.op("dve", lambda e: e.tensor_scalar(out=lb[:], in0=lb[:], scalar1=1.0, scalar2=None, op0=ALU.add), deps=[a1])
        a3 = P.op("dve", lambda e: e.reciprocal(out=lb[:], in_=lb[:]), deps=[a2])
        lb_ready = P.op("dve", lambda e: e.tensor_scalar(out=oml[:], in0=lb[:], scalar1=-1.0, scalar2=1.0, op0=ALU.mult, op1=ALU.add),
                        deps=[a3])

    N = 2048
    Ar = Ring([P.sbuf([128, 4, 512], F32, f"A{i}") for i in range(2)])
    qtr = Ring([P.sbuf([128, 4, 512], F32, f"qt{i}") for i in range(2)])
    szr = Ring([P.sbuf([128, 4, 512], F32, f"szt{i}") for i in range(2)])
    ebr = Ring([P.sbuf([128, 4, 512], F32, f"eb{i}") for i in range(2)])
    qbr = Ring([P.sbuf([128, 4, 512], BF16, f"qb{i}") for i in range(2)])
    kbr = Ring([P.sbuf([128, 4, 512], BF16, f"kb{i}") for i in range(2)])
    Bt = P.sbuf([128, 4, 512], F32, "Bt")
    Ct = P.sbuf([128, 4, 512], F32, "Ct")
    Dn = P.sbuf([128, 4, 512], F32, "Dn")
    ivr = Ring([P.sbuf([128, 4, 128], BF16, f"ivt{i}") for i in range(3)])
    kbmr = Ring([P.sbuf([128, 4, 128], BF16, f"kbm{i}") for i in range(8)])
    attr = Ring([P.sbuf([128, 128], BF16, f"attT{i}") for i in range(3)])
    G = [[P.sbuf([128, 128], F32, f"G{h}_{k}") for k in range(2)] for h in range(4)]
    sbf = [Ring([P.sbuf([128, 128], BF16, f"sbf{h}_{k}") for k in range(2)]) for h in range(4)]
    o_sb = P.sbuf([128, 512], F32, "o_sb")
    osq = P.sbuf([128, 512], F32, "osq")
    rst = P.sbuf([128, 512], F32, "rst")
    o16r = Ring([P.sbuf([128, 4, 128], BF16, f"o16_{i}") for i in range(2)])

    tb = P.psum([128, 1024], BF16, "tb")
    att_r = Ring([P.psum([128, 512], F32, f"attp{i}") for i in range(2)])
    oT_r = Ring([P.psum([128, 512], F32, f"oTp{i}") for i in range(2)])
    U_r = Ring([P.psum([128, 512], F32, f"Up{i}") for i in range(2)])
    ss_ps = P.psum([128, 512], F32, "ssp")

    fl = lambda t: t[:].rearrange("p h n -> p (h n)")
    st = {"dve_last": None, "act_last": None, "Gprod": [None] * 4, "Gidx": [0] * 4, "sbf_act": [None] * 4,
          "prev_ebl": [None] * 4, "prev_eb_op": None, "tb_reader": None, "ss_reader": None, "fin_last": None,
          "o_sb_readers": [], "out_dmas": [], "BtCt_readers": []}
    chunk_global = 0
    for T in range(nst):
        t0 = T * 512
        ka_, A, rdA = Ar.next()
        d_f = P.dma("sp", A[:], fT_d[:, :, t0:t0 + 512], deps=rdA)
        kq_, qt, rdq = qtr.next()
        d_q = P.dma("sp", qt[:], qT_d[:, :, t0:t0 + 512], deps=rdq)
        kz_, szt, rdz = szr.next()
        d_z = P.dma("sp", szt[:], szT_d[:, :, t0:t0 + 512], deps=rdz)
        e1 = P.op("act", lambda e, A=A: e.activation(out=fl(A), in_=fl(A), func=AF.Exp, scale=-1.0), deps=[d_f])
        v1 = P.op("dve", lambda e, A=A: e.tensor_scalar(out=fl(Bt), in0=fl(A), scalar1=1.0, scalar2=None, op0=ALU.add),
                  deps=[e1] + st["BtCt_readers"])
        v2 = P.op("dve", lambda e: e.reciprocal(out=fl(Bt), in_=fl(Bt)), deps=[v1])
        v3 = P.op("dve", lambda e, A=A: e.tensor_tensor(out=fl(A), in0=fl(A), in1=fl(Bt), op=ALU.mult), deps=[v2])
        lastv = v3
        if layer_j >= 1:
            for h in range(4):
                lastv = P.op("dve", lambda e, A=A, h=h: e.tensor_scalar(out=A[:, h, :], in0=A[:, h, :], scalar1=oml[:, h:h + 1],
                                                                        scalar2=None, op0=ALU.mult), deps=[lastv, lb_ready])
            for h in range(4):
                lastv = P.op("dve", lambda e, h=h: e.tensor_scalar(out=Bt[:, h, :], in0=Bt[:, h, :], scalar1=oml[:, h:h + 1],
                                                                   scalar2=lb[:, h:h + 1], op0=ALU.mult, op1=ALU.add), deps=[lastv])
        e2 = P.op("act", lambda e: e.activation(out=fl(Ct), in_=fl(Bt), func=AF.Ln), deps=[lastv])
        v4 = P.op("dve", lambda e: e.tensor_tensor_scan(out=fl(Bt), data0=seg[:], data1=fl(Ct), initial=0.0,
                                                        op0=ALU.mult, op1=ALU.add), deps=[e2, ld["seg"]])
        ke_, eb, rde = ebr.next()
        e3 = P.op("act", lambda e, eb=eb: e.activation(out=fl(eb), in_=fl(Bt), func=AF.Exp), deps=[v4] + rde)
        e4 = P.op("act", lambda e: e.activation(out=fl(Dn), in_=fl(Bt), func=AF.Exp, scale=-1.0), deps=[v4, st["dve_last"]])
        kqb_, qb, rdqb = qbr.next()
        v5 = P.op("dve", lambda e, qb=qb, qt=qt, eb=eb: e.tensor_tensor(out=fl(qb), in0=fl(qt), in1=fl(eb), op=ALU.mult),
                  deps=[e3, d_q] + rdqb)
        kkb_, kb, rdkb = kbr.next()
        v6 = P.op("dve", lambda e, kb=kb, A=A: e.tensor_tensor(out=fl(kb), in0=fl(A), in1=fl(Dn), op=ALU.mult),
                  deps=[e4, lastv] + rdkb)
        st["dve_last"] = v6
        st["BtCt_readers"] = [e3, e4, v4]
        Ar.read_by(ka_, v6)
        qtr.read_by(kq_, v5)
        qk_ready = [v5, v6]

        for tt in range(4):
            o = tt * 128
            J = T * 4 + tt
            kiv_, ivt, rdiv = ivr.next()
            d_iv = P.dma("sp", ivt[:], iv_d[:, J, :, :], deps=rdiv)
            ko_, oTp, rdo = oT_r.next()
            kbms = []
            last_pe = None
            for h in range(4):
                trp = P.op("pe", lambda e, kb=kb, h=h, o=o: e.transpose(out=tb[:, 0:128], in_=kb[:, h, o:o + 128], identity=identb[:]),
                           deps=qk_ready + [ld["idb"], st["tb_reader"]])
                kk_, kbm, rdk = kbmr.next()
                lastk = None
                for c in range(4):
                    lastk = P.op("act", lambda e, kbm=kbm, c=c: e.activation(out=kbm[:, c, :], in_=tb[:, 0:128], func=AF.Copy,
                                                                              scale=cm[:, c:c + 1]), deps=[trp, ld["cm"]] + rdk)
                st["tb_reader"] = lastk
                kbms.append((kk_, kbm, lastk))
                kat_, attp, rdat = att_r.next()
                ma = P.op("pe", lambda e, attp=attp, kb=kb, qb=qb, h=h, o=o: e.matmul(
                    attp[:, 0:128], lhsT=kb[:, h, o:o + 128], rhs=qb[:, h, o:o + 128], start=True, stop=True),
                    deps=qk_ready + rdat)
                ks_, attT, rds = attr.next()
                va = P.op("dve", lambda e, attT=attT, attp=attp: e.tensor_tensor(out=attT[:], in0=attp[:, 0:128], in1=mbd[:], op=ALU.mult),
                          deps=[ma, ld["mbd"]] + rds)
                att_r.read_by(kat_, va)
                mi = P.op("pe", lambda e, oTp=oTp, ivt=ivt, attT=attT, h=h: e.matmul(
                    oTp[:, h * 128:(h + 1) * 128], lhsT=ivt[:, h, :], rhs=attT[:], start=(h == 0), stop=False),
                    deps=[va, d_iv] + rdo)
                attr.read_by(ks_, mi)
                last_pe = mi
            for c in range(4):
                for h in range(4):
                    first_chunk = (chunk_global == 0)
                    gi = st["Gidx"][h]
                    Gold, Gnew = G[h][gi], G[h][1 - gi]
                    if not first_chunk:
                        ebl, ebop = st["prev_ebl"][h]
                        ksb_, sb_, rdsb = sbf[h].next()
                        sa = P.op("act", lambda e, sb_=sb_, Gold=Gold, ebl=ebl: e.activation(out=sb_[:], in_=Gold[:], func=AF.Copy, scale=ebl),
                                  deps=[st["Gprod"][h], ebop] + rdsb)
                        mint = P.op("pe", lambda e, oTp=oTp, sb_=sb_, qb=qb, h=h, o=o, c=c: e.matmul(
                            oTp[:, h * 128 + 32 * c:h * 128 + 32 * c + 32], lhsT=sb_[:], rhs=qb[:, h, o + 32 * c:o + 32 * c + 32],
                            start=False, stop=(c == 3)), deps=[sa])
                        sbf[h].read_by(ksb_, mint)
                        last_pe = mint
                    ku_, Up, rdu = U_r.next()
                    kk_, kbm, lastk = kbms[h]
                    mu = P.op("pe", lambda e, Up=Up, kbm=kbm, ivt=ivt, h=h, c=c: e.matmul(
                        Up[:, 0:128], lhsT=kbm[:, c, :], rhs=ivt[:, h, :], start=True, stop=True), deps=[lastk, d_iv] + rdu)
                    last_pe = mu
                    if first_chunk:
                        gp = P.op("dve", lambda e, Gnew=Gnew, Up=Up: e.tensor_copy(out=Gnew[:], in_=Up[:, 0:128]), deps=[mu])
                    else:
                        gp = P.op("dve", lambda e, Gnew=Gnew, Gold=Gold, Up=Up, ebl=ebl: e.scalar_tensor_tensor(
                            out=Gnew[:], in0=Gold[:], scalar=ebl, in1=Up[:, 0:128], op0=ALU.mult, op1=ALU.add),
                            deps=[mu, st["Gprod"][h], ebop, st["sbf_act"][h]])
                        st["sbf_act"][h] = sa
                    U_r.read_by(ku_, gp)
                    st["Gprod"][h] = gp
                    st["Gidx"][h] = 1 - gi
                    col = o + 32 * c + 31
                    st["prev_ebl"][h] = (eb[:, h, col:col + 1], e3)
                    if h == 3:
                        chunk_global += 1
                for h in range(4):
                    if c == 3:
                        kbmr.read_by(kbms[h][0], last_pe)
            ivr.read_by(kiv_, last_pe)
            f0 = P.op("dve", lambda e, oTp=oTp: e.tensor_copy(out=o_sb[:], in_=oTp[:]), deps=[last_pe] + st["o_sb_readers"])
            oT_r.read_by(ko_, f0)
            f1 = P.op("act", lambda e: e.activation(out=osq[:], in_=o_sb[:], func=AF.Square), deps=[f0, st["fin_last"]])
            f2 = P.op("pe", lambda e: e.matmul(ss_ps[:], lhsT=ones[:], rhs=osq[:], start=True, stop=True),
                      deps=[f1, ld["ones"], st["ss_reader"]])
            f3 = P.op("dve", lambda e: e.tensor_scalar(out=rst[:], in0=ss_ps[:], scalar1=1.0 / 128, scalar2=RMS_EPS, op0=ALU.mult, op1=ALU.add),
                      deps=[f2, st["fin_last"]])
            st["ss_reader"] = f3
            f4 = P.op("act", lambda e: e.activation(out=rst[:], in_=rst[:], func=AF.Ln), deps=[f3])
            f5 = P.op("act", lambda e: e.activation(out=rst[:], in_=rst[:], func=AF.Exp, scale=-0.5), deps=[f4])
            f6 = P.op("dve", lambda e: e.tensor_tensor(out=o_sb[:], in0=o_sb[:], in1=rst[:], op=ALU.mult), deps=[f5, f1])
            k16_, o16, rd16 = o16r.next()
            f7 = P.op("dve", lambda e, o16=o16, szt=szt, o=o: e.scalar_tensor_tensor(
                out=o16[:], in0=o_sb[:].rearrange("p (h n) -> p h n", h=4), scalar=gw[:, 0:1], in1=szt[:, :, o:o + 128],
                op0=ALU.mult, op1=ALU.mult), deps=[f6, d_z, ld["gw"]] + rd16)
            st["fin_last"] = f7
            st["o_sb_readers"] = [f7]
            od = P.dma("pool", oT_d[:, :, t0 + o:t0 + o + 128], o16[:], deps=[f7])
            o16r.read_by(k16_, od)
            st["out_dmas"].append(od)
        szr.read_by(kz_, st["fin_last"])
        qbr.read_by(kqb_, last_pe)
        kbr.read_by(kkb_, last_pe)
        ebr.read_by(ke_, st["Gprod"][3], st["sbf_act"][3] if st["sbf_act"][3] is not None else st["Gprod"][3])
    P.wait_all("pool", st["out_dmas"])
    P.emit()
    return nc


def _blocks_cols(c0, ncols, kind, oname, oc0, nb=512):
    out = []
    o = 0
    while o < ncols:
        n = min(nb, ncols - o)
        out.append((c0 + o, n, kind, oname, oc0 + o))
        o += n
    return out


_CACHE = {}


def _get(key, fn):
    return fn()


def _inv_freq():
    half = 16
    return (np.float32(ROPE_THETA) ** (-np.arange(half, dtype=np.float32) / np.float32(half))).astype(np.float32)


def run_nsa_proj(x2d, pos1d, norm_w, w_in):
    blocks = (_blocks_cols(0, 2048, "rope", "qkv", 0) + _blocks_cols(2048, 512, "rope", "qkv", 2048)
              + _blocks_cols(2560, 512, "bf16", "qkv", 2560) + _blocks_cols(3072, 512, "rope", "qkv", 3072)
              + _blocks_cols(3584, 512, "bf16", "qkv", 3584) + _blocks_cols(4096, 512, "rope", "qkv", 4096)
              + _blocks_cols(4608, 512, "bf16", "qkv", 4608) + _blocks_cols(5120, 48, "sigmoid", "gate", 0)
              + _blocks_cols(5168, 2048, "silu", "sz", 0))
    outs = {"qkv": (5120, BF16), "gate": (48, F32), "sz": (2048, F32)}
    nc = _get("nsa_proj", lambda: build_proj(7216, blocks, "norm", outs, True))
    nw = np.ascontiguousarray(norm_w.reshape(16, 128).T)
    ident = np.eye(128, dtype=np.float32)
    invf = np.ascontiguousarray(np.broadcast_to(_inv_freq()[None, :], (128, 16)))
    in_maps = []
    for c in range(NCORE):
        sl = slice(c * TPC, (c + 1) * TPC)
        in_maps.append({"x": np.ascontiguousarray(x2d[sl]), "w": w_in, "nw": nw, "ident": ident,
                        "pos": np.ascontiguousarray(pos1d[sl].reshape(TPC // 128, 128).T), "invf": invf})
    res = run_bass_kernel_spmd(nc, in_maps, core_ids=list(range(NCORE)))
    qkv = np.concatenate([r["qkv"] for r in res.results], axis=0)
    gate = np.concatenate([r["gate"] for r in res.results], axis=0)
    sz = np.concatenate([r["sz"] for r in res.results], axis=0)
    return qkv, gate, sz


def _nsa_consts():
    k = np.arange(SEQ)
    E = (k[None, :] // 64 == np.arange(128)[:, None]).astype(np.float32).astype(NPBF)
    n = np.arange(512)
    blk = np.arange(128)
    ov = ((16 * n[:, None] < (blk[None, :] + 1) * 64) & (16 * n[:, None] + 32 > blk[None, :] * 64)).astype(np.float32)
    ov[511] = 0
    ov = np.ascontiguousarray(ov.reshape(4, 128, 128).transpose(1, 0, 2)).astype(NPBF)
    identb = np.eye(128, dtype=np.float32).astype(NPBF)
    return E, ov, identb


def nsa_core_inmaps(qkv, gate, sz, ck_pos, ck_w1, ck_w2, cv_pos, cv_w1, cv_w2):
    E, ov, identb = _nsa_consts()
    w1k = np.ascontiguousarray(ck_w1.transpose(1, 0, 2))
    w1v = np.ascontiguousarray(cv_w1.transpose(1, 0, 2))
    posk = np.ascontiguousarray(ck_pos.T)
    posv = np.ascontiguousarray(cv_pos.T)
    in_maps = []
    for c in range(NCORE):
        b, g = divmod(c, 4)
        rows = slice(b * SEQ, (b + 1) * SEQ)
        qk = qkv[rows]

        def colT(c0):
            return np.ascontiguousarray(qk[:, c0 + g * 128:c0 + (g + 1) * 128].T)

        def tokmaj(c0):
            return np.ascontiguousarray(qk[:, c0 + g * 128:c0 + (g + 1) * 128].reshape(64, 128, 128).transpose(1, 0, 2))

        qT = np.ascontiguousarray(qk[:, g * 512:(g + 1) * 512].reshape(SEQ, 4, 128).transpose(2, 1, 0))
        gt = gate[rows].reshape(SEQ, 3, 16)[:, :, 4 * g:4 * g + 4].reshape(SEQ, 12)
        gt = np.ascontiguousarray(gt.reshape(64, 128, 12).transpose(1, 0, 2))
        szz = np.ascontiguousarray(sz[rows][:, g * 512:(g + 1) * 512].reshape(64, 128, 512).transpose(1, 0, 2))
        in_maps.append({"qT": qT, "kcT": colT(2048), "vcT": colT(2560), "ksT": colT(3072), "kwT": colT(4096),
                        "vs": tokmaj(3584), "vw": tokmaj(4608), "gate": gt, "sz": szz,
                        "w1k": w1k, "w1v": w1v, "posk": posk, "posv": posv, "w2k": ck_w2, "w2v": cv_w2,
                        "E": E, "ov": ov, "identb": identb})
    return in_maps


def run_nsa_core(qkv, gate, sz, ck_pos, ck_w1, ck_w2, cv_pos, cv_w1, cv_w2, nqb=64):
    nc = _get(("nsa_core", nqb), lambda: build_nsa_core(nqb))
    in_maps = nsa_core_inmaps(qkv, gate, sz, ck_pos, ck_w1, ck_w2, cv_pos, cv_w1, cv_w2)
    res = run_bass_kernel_spmd(nc, in_maps, core_ids=list(range(NCORE)))
    o = np.zeros((NTOK, 2048), dtype=NPBF)
    for c in range(NCORE):
        b, g = divmod(c, 4)
        o[b * SEQ:(b + 1) * SEQ, g * 512:(g + 1) * 512] = res.results[c]["o"]
    return o


def run_hgrn_proj(x2d, norm_w, w_in):
    blocks = (_blocks_cols(0, 2048, "silu", "qs", 0) + _blocks_cols(2048, 2048, "f32", "fr", 0)
              + _blocks_cols(4096, 2048, "bf16", "iv", 0) + _blocks_cols(6144, 2048, "silu", "sz", 0))
    outs = {"qs": (2048, F32), "fr": (2048, F32), "iv": (2048, BF16), "sz": (2048, F32)}
    nc = _get("hgrn_proj", lambda: build_proj(8192, blocks, "norm", outs, False))
    nw = np.ascontiguousarray(norm_w.reshape(16, 128).T)
    ident = np.eye(128, dtype=np.float32)
    in_maps = [{"x": np.ascontiguousarray(x2d[c * TPC:(c + 1) * TPC]), "w": w_in, "nw": nw, "ident": ident} for c in range(NCORE)]
    res = run_bass_kernel_spmd(nc, in_maps, core_ids=list(range(NCORE)))
    return tuple(np.concatenate([r[k] for r in res.results], axis=0) for k in ("qs", "fr", "iv", "sz"))


def _hgrn_consts():
    s = np.arange(128)
    mbd = ((s[:, None] // 32 == s[None, :] // 32) & (s[:, None] <= s[None, :])).astype(np.float32)
    cm = (s[:, None] // 32 == np.arange(4)[None, :]).astype(np.float32)
    seg = np.ones((128, 2048), np.float32)
    seg[:, ::32] = 0.0
    ones = np.ones((128, 128), np.float32)
    identb = np.eye(128, dtype=np.float32).astype(NPBF)
    return mbd, cm, seg, ones, identb


def hgrn_core_inmaps(qs, fr, iv, sz, lb_logits, gnorm_w):
    mbd, cm, seg, ones, identb = _hgrn_consts()
    in_maps = []
    for c in range(NCORE):
        b, hg = divmod(c, 4)
        rows = slice(b * SEQ, (b + 1) * SEQ)
        cols = slice(hg * 512, (hg + 1) * 512)

        def featmaj(a):
            return np.ascontiguousarray(a[rows, cols].reshape(SEQ, 4, 128).transpose(2, 1, 0))

        ivv = np.ascontiguousarray(iv[rows, cols].reshape(64, 128, 4, 128).transpose(1, 0, 2, 3))
        lbl = np.ascontiguousarray(lb_logits[:, cols].reshape(2, 4, 128).transpose(2, 1, 0))
        in_maps.append({"qT": featmaj(qs), "fT": featmaj(fr), "szT": featmaj(sz), "iv": ivv, "lbl": lbl,
                        "gw": np.ascontiguousarray(gnorm_w.reshape(128, 1)), "maskbd": mbd, "cm": cm, "seg": seg,
                        "ones": ones, "identb": identb})
    return in_maps


def run_hgrn_core(qs, fr, iv, sz, lb_logits, gnorm_w, layer_j, nst=16):
    nc = _get(("hgrn_core", layer_j, nst), lambda: build_hgrn_core(layer_j, nst))
    in_maps = hgrn_core_inmaps(qs, fr, iv, sz, lb_logits, gnorm_w)
    res = run_bass_kernel_spmd(nc, in_maps, core_ids=list(range(NCORE)))
    o = np.zeros((NTOK, 2048), dtype=NPBF)
    for c in range(NCORE):
        b, hg = divmod(c, 4)
        o[b * SEQ:(b + 1) * SEQ, hg * 512:(hg + 1) * 512] = res.results[c]["oT"].transpose(2, 1, 0).reshape(SEQ, 512)
    return o


def run_outproj(o_bf16, x2d, w_out):
    blocks = _blocks_cols(0, 2048, "resid", "y", 0)
    nc = _get("outproj", lambda: build_proj(2048, blocks, "lhsT", {"y": (2048, F32)}, False))
    in_maps = []
    for c in range(NCORE):
        sl = slice(c * TPC, (c + 1) * TPC)
        oT = np.ascontiguousarray(o_bf16[sl].T.reshape(16, 128, TPC).transpose(1, 0, 2))
        in_maps.append({"x": np.ascontiguousarray(x2d[sl]), "w": w_out, "aT": oT})
    res = run_bass_kernel_spmd(nc, in_maps, core_ids=list(range(NCORE)))
    return np.concatenate([r["y"] for r in res.results], axis=0)


def build_final_norm():
    nc = bass.Bass("TRN2", target_bir_lowering=False)
    P = Prog(nc)
    NT = TPC // 128
    x = nc.dram_tensor("x", [TPC, D_MODEL], F32, kind="ExternalInput").ap()
    fwd = nc.dram_tensor("fw", [128, D_MODEL], F32, kind="ExternalInput").ap()
    y = nc.dram_tensor("y", [TPC, D_MODEL], F32, kind="ExternalOutput").ap()
    fw = P.sbuf([128, D_MODEL], F32, "fw")
    d_fw = P.dma("act", fw[:], fwd)
    ss = P.sbuf([128, NT], F32, "ss")
    rstd = P.sbuf([128, NT], F32, "rstd")
    junk = P.sbuf([128, D_MODEL], BF16, "junk")
    xr = Ring([P.sbuf([128, D_MODEL], F32, f"xt{i}") for i in range(3)])
    yr = Ring([P.sbuf([128, D_MODEL], F32, f"yt{i}") for i in range(3)])
    outs = []
    for t in range(NT):
        k, xt, rd = xr.next()
        d_x = P.dma("sp", xt[:], x[t * 128:(t + 1) * 128, :], deps=rd)
        a_sq = P.op("act", lambda e, xt=xt, t=t: e.activation(out=junk[:], in_=xt[:], func=AF.Square, accum_out=ss[:, t:t + 1]), deps=[d_x])
        v1 = P.op("dve", lambda e, t=t: e.tensor_scalar(out=rstd[:, t:t + 1], in0=ss[:, t:t + 1], scalar1=1.0 / D_MODEL, scalar2=RMS_EPS,
                                                        op0=ALU.mult, op1=ALU.add), deps=[a_sq])
        a_rt = P.op("act", lambda e, t=t: e.activation(out=rstd[:, t:t + 1], in_=rstd[:, t:t + 1], func=AF.Sqrt), deps=[v1])
        v2 = P.op("dve", lambda e, t=t: e.reciprocal(out=rstd[:, t:t + 1], in_=rstd[:, t:t + 1]), deps=[a_rt])
        ky, yt, rdy = yr.next()
        v3 = P.op("dve", lambda e, yt=yt, xt=xt, t=t: e.scalar_tensor_tensor(out=yt[:], in0=xt[:], scalar=rstd[:, t:t + 1], in1=fw[:],
                                                                             op0=ALU.mult, op1=ALU.mult), deps=[v2, d_x, d_fw] + rdy)
        xr.read_by(k, a_sq, v3)
        od = P.dma("pool", y[t * 128:(t + 1) * 128, :], yt[:], deps=[v3])
        yr.read_by(ky, od)
        outs.append(od)
    P.wait_all("pool", outs)
    P.emit()
    return nc


def run_final_norm(x2d, fw):
    nc = _get("final_norm", build_final_norm)
    fwb = np.ascontiguousarray(np.broadcast_to(fw[None, :], (128, D_MODEL)))
    in_maps = [{"x": np.ascontiguousarray(x2d[c * TPC:(c + 1) * TPC]), "fw": fwb} for c in range(NCORE)]
    res = run_bass_kernel_spmd(nc, in_maps, core_ids=list(range(NCORE)))
    return np.concatenate([r["y"] for r in res.results], axis=0)


def kernel(x, positions, norm_w, final_norm_w, nsa_w_in, nsa_ck_pos, nsa_ck_w1, nsa_ck_w2,
           nsa_cv_pos, nsa_cv_w1, nsa_cv_w2, nsa_w_out, hgrn_w_in, hgrn_lb_logits, hgrn_gnorm_w, hgrn_w_out):
    f = lambda a: np.ascontiguousarray(np.asarray(a, dtype=np.float32))
    x2d = f(x).reshape(NTOK, D_MODEL)
    pos1d = np.ascontiguousarray(np.asarray(positions).astype(np.int32).reshape(-1))
    norm_w = f(norm_w)
    for layer in range(4):
        j = layer // 2
        if layer % 2 == 0:
            qkv, gate, sz = run_nsa_proj(x2d, pos1d, norm_w[layer], f(nsa_w_in[j]))
            o = run_nsa_core(qkv, gate, sz, f(nsa_ck_pos[j]), f(nsa_ck_w1[j]), f(nsa_ck_w2[j]),
                             f(nsa_cv_pos[j]), f(nsa_cv_w1[j]), f(nsa_cv_w2[j]))
            x2d = run_outproj(o, x2d, f(nsa_w_out[j]))
        else:
            qs, fr, iv, sz = run_hgrn_proj(x2d, norm_w[layer], f(hgrn_w_in[j]))
            o = run_hgrn_core(qs, fr, iv, sz, f(hgrn_lb_logits), f(hgrn_gnorm_w[j]), j)
            x2d = run_outproj(o, x2d, f(hgrn_w_out[j]))
    out = run_final_norm(x2d, f(final_norm_w))
    return out.reshape(BATCH, SEQ, D_MODEL)
```

```python
import contextlib
import math
import numpy as np
import ml_dtypes
import concourse.bass as bass
import concourse.mybir as mybir
from concourse.bass_utils import run_bass_kernel_spmd

F32 = mybir.dt.float32
BF16 = mybir.dt.bfloat16
I32 = mybir.dt.int32
AF = mybir.ActivationFunctionType
ALU = mybir.AluOpType
AX = mybir.AxisListType
NPBF = ml_dtypes.bfloat16

D_MODEL = 2048
BATCH = 2
SEQ = 8192
NTOK = BATCH * SEQ
NCORE = 8
TPC = NTOK // NCORE
RMS_EPS = 1e-6
ROPE_THETA = 500000.0

ENGS = ("pe", "act", "dve", "pool", "sp")
SEM_CH = 12000
N_DMA_SEMS = 8


class Op:
    __slots__ = ("eng", "fn", "deps", "needs_inc", "is_dma", "sem", "token", "prev")

    def __init__(self, eng, fn, deps, is_dma):
        self.eng = eng
        self.fn = fn
        self.deps = [d for d in deps if d is not None]
        self.needs_inc = False
        self.is_dma = is_dma
        self.sem = None
        self.token = None
        self.prev = 0


class Prog:
    def __init__(self, nc):
        self.nc = nc
        self.q = {e: [] for e in ENGS}
        self.stack = contextlib.ExitStack()
        self._n = 0

    def sbuf(self, shape, dtype, name=None):
        self._n += 1
        return self.stack.enter_context(self.nc.sbuf_tensor((name + "_sb") if name else f"sb{self._n}", list(shape), dtype))

    def psum(self, shape, dtype=F32, name=None):
        self._n += 1
        return self.stack.enter_context(self.nc.psum_tensor(name or f"ps{self._n}", list(shape), dtype))

    def sem(self, name=None):
        self._n += 1
        return self.stack.enter_context(self.nc.semaphore(name or f"sem{self._n}"))

    def op(self, eng, fn, deps=()):
        o = Op(eng, fn, deps, False)
        for d in o.deps:
            d.needs_inc = True
        self.q[eng].append(o)
        return o

    def dma(self, eng, out, in_, deps=(), **kw):
        o = Op(eng, lambda e: e.dma_start(out=out, in_=in_, **kw), deps, True)
        for d in o.deps:
            d.needs_inc = True
        o.needs_inc = True
        self.q[eng].append(o)
        return o

    def wait_all(self, eng, deps):
        return self.op(eng, None, deps)

    def emit(self):
        nc = self.nc
        eng_sems = {e: [] for e in ENGS}
        dma_sems = {e: [self.sem(f"dq_{e}_{i}") for i in range(N_DMA_SEMS)] for e in ("sp", "act", "pool")}
        dma_cnt = {}
        for e in ENGS:
            cnt = 0
            ndma = 0
            for o in self.q[e]:
                if o.is_dma:
                    s = dma_sems[e][ndma % N_DMA_SEMS]
                    ndma += 1
                    prev = dma_cnt.get(id(s), 0)
                    o.sem = s
                    o.token = prev + 16
                    o.prev = prev
                    dma_cnt[id(s)] = prev + 16
                elif o.needs_inc:
                    k = cnt // SEM_CH
                    while len(eng_sems[e]) <= k:
                        eng_sems[e].append(self.sem(f"pg_{e}_{len(eng_sems[e])}"))
                    o.sem = eng_sems[e][k]
                    o.token = cnt % SEM_CH + 1
                    cnt += 1
        engmap = {"pe": "tensor", "act": "scalar", "dve": "vector", "pool": "gpsimd", "sp": "sync"}
        with nc.Block() as block:
            for e in ENGS:
                ops = self.q[e]
                if not ops:
                    continue

                def body(eng, ops=ops):
                    waited = {}
                    for o in ops:
                        need = {}
                        for d in o.deps:
                            key = id(d.sem)
                            if waited.get(key, 0) >= d.token:
                                continue
                            if key not in need or need[key][1] < d.token:
                                need[key] = (d.sem, d.token)
                        if o.is_dma and o.prev > 0 and waited.get(id(o.sem), 0) < o.prev:
                            key = id(o.sem)
                            if key not in need or need[key][1] < o.prev:
                                need[key] = (o.sem, o.prev)
                        for key, (s, v) in need.items():
                            eng.wait_ge(s, v)
                            waited[key] = v
                        if o.fn is None:
                            continue
                        ins = o.fn(eng)
                        if o.is_dma:
                            ins.then_inc(o.sem, 16)
                        elif o.needs_inc:
                            ins.then_inc(o.sem, 1)

                getattr(block, engmap[e])(body)
        self.stack.close()


class Ring:
    def __init__(self, bufs):
        self.bufs = bufs
        self.i = 0
        self.readers = [[] for _ in bufs]

    def next(self):
        k = self.i % len(self.bufs)
        self.i += 1
        r = self.readers[k]
        self.readers[k] = []
        return k, self.bufs[k], r

    def read_by(self, k, *ops):
        self.readers[k].extend(ops)


def build_proj(nout, blocks, mode, outs, rope, dump_hn=False, nt_limit=None):
    nc = bass.Bass("TRN2", target_bir_lowering=False)
    P = Prog(nc)
    NT = TPC // 128
    KC = D_MODEL // 128
    w = nc.dram_tensor("w", [D_MODEL, nout], F32, kind="ExternalInput").ap()
    wv = w.rearrange("(kc p) n -> p kc n", p=128)
    need_x = (mode == "norm") or any(b[2] == "resid" for b in blocks)
    if need_x:
        x = nc.dram_tensor("x", [TPC, D_MODEL], F32, kind="ExternalInput").ap()
    if mode == "norm":
        nw = nc.dram_tensor("nw", [128, KC], F32, kind="ExternalInput").ap()
        ident_d = nc.dram_tensor("ident", [128, 128], F32, kind="ExternalInput").ap()
    else:
        aT = nc.dram_tensor("aT", [128, KC, TPC], BF16, kind="ExternalInput").ap()
    if rope:
        pos = nc.dram_tensor("pos", [128, NT], I32, kind="ExternalInput").ap()
        invf = nc.dram_tensor("invf", [128, 16], F32, kind="ExternalInput").ap()
    out_aps = {k: nc.dram_tensor(k, [TPC, v[0]], v[1], kind="ExternalOutput").ap() for k, v in outs.items()}

    hnT = P.sbuf([128, KC, TPC], BF16, "hnT")
    out_dmas = []
    if dump_hn:
        hn_out = nc.dram_tensor("hn", [128, KC, TPC], BF16, kind="ExternalOutput").ap()

    if mode == "norm":
        nw_sb = P.sbuf([128, KC], F32, "nw_sb")
        ident = P.sbuf([128, 128], F32, "ident_sb")
        d_nw = P.dma("sp", nw_sb[:], nw)
        d_id = P.dma("sp", ident[:], ident_d)
        ss = P.sbuf([128, NT], F32, "ss")
        rstd = P.sbuf([128, NT], F32, "rstd")
        junk = P.sbuf([128, D_MODEL], BF16, "junk")
        xr = Ring([P.sbuf([128, D_MODEL], F32, f"xt{i}") for i in range(2)])
        xsr = Ring([P.sbuf([128, D_MODEL], F32, f"xs{i}") for i in range(2)])
        tpr = Ring([P.psum([128, 512], F32, f"tp{i}") for i in range(2)])
        junk_last = None
        for t in range(NT if nt_limit is None else nt_limit):
            k, xt, rd = xr.next()
            d_x = P.dma("sp", xt[:], x[t * 128:(t + 1) * 128, :], deps=rd)
            a_sq = P.op("act", lambda e, xt=xt, t=t: e.activation(out=junk[:], in_=xt[:], func=AF.Square,
                                                                 accum_out=ss[:, t:t + 1]), deps=[d_x])
            junk_last = a_sq
            v1 = P.op("dve", lambda e, t=t: e.tensor_scalar(out=rstd[:, t:t + 1], in0=ss[:, t:t + 1],
                                                            scalar1=1.0 / D_MODEL, scalar2=RMS_EPS,
                                                            op0=ALU.mult, op1=ALU.add), deps=[a_sq])
            a_rt = P.op("act", lambda e, t=t: e.activation(out=rstd[:, t:t + 1], in_=rstd[:, t:t + 1], func=AF.Sqrt),
                        deps=[v1])
            v2 = P.op("dve", lambda e, t=t: e.reciprocal(out=rstd[:, t:t + 1], in_=rstd[:, t:t + 1]), deps=[a_rt])
            ks, xs, rds = xsr.next()
            v3 = P.op("dve", lambda e, xs=xs, xt=xt, t=t: e.tensor_scalar(out=xs[:], in0=xt[:], scalar1=rstd[:, t:t + 1],
                                                                          scalar2=None, op0=ALU.mult),
                      deps=[v2, d_x] + rds)
            xr.read_by(k, a_sq, v3)
            for g in range(KC // 4):
                kp, tp, rdp = tpr.next()
                last = None
                for j in range(4):
                    kc = g * 4 + j
                    last = P.op("pe", lambda e, tp=tp, xs=xs, kc=kc, j=j: e.transpose(
                        out=tp[:, j * 128:(j + 1) * 128], in_=xs[:, kc * 128:(kc + 1) * 128], identity=ident[:]),
                        deps=[v3, d_id] + rdp)
                evs = []
                for j in range(4):
                    kc = g * 4 + j
                    eng = "dve" if kp == 0 else "act"
                    if eng == "dve":
                        ev = P.op("dve", lambda e, tp=tp, kc=kc, j=j, t=t: e.tensor_scalar(
                            out=hnT[:, kc, t * 128:(t + 1) * 128], in0=tp[:, j * 128:(j + 1) * 128],
                            scalar1=nw_sb[:, kc:kc + 1], scalar2=None, op0=ALU.mult), deps=[last, d_nw])
                    else:
                        ev = P.op("act", lambda e, tp=tp, kc=kc, j=j, t=t: e.activation(
                            out=hnT[:, kc, t * 128:(t + 1) * 128], in_=tp[:, j * 128:(j + 1) * 128],
                            func=AF.Copy, scale=nw_sb[:, kc:kc + 1]), deps=[last, d_nw])
                    evs.append(ev)
                tpr.read_by(kp, *evs)
                if g == KC // 4 - 1:
                    xsr.read_by(ks, last)
            hn_ready_t = evs
        hn_ready = [P.q["dve"][-1], P.q["act"][-1]]
        for o in hn_ready:
            o.needs_inc = True
    else:
        hn_ready = []
        for kc0 in range(0, KC, 4):
            hn_ready.append(P.dma("sp", hnT[:, kc0:kc0 + 4, :], aT[:, kc0:kc0 + 4, :]))

    if dump_hn:
        out_dmas.append(P.dma("sp", hn_out, hnT[:], deps=hn_ready))
    if rope:
        posi = P.sbuf([128, NT], I32, "posi")
        posf = P.sbuf([128, NT], F32, "posf")
        invf_sb = P.sbuf([128, 16], F32, "invf_sb")
        ang = P.sbuf([128, NT, 16], F32, "ang")
        kq = P.sbuf([128, NT, 16], F32, "kq")
        kqi = P.sbuf([128, NT, 16], I32, "kqi")
        red = P.sbuf([128, NT, 16], F32, "red")
        sarg = P.sbuf([128, NT, 16], F32, "sarg")
        carg = P.sbuf([128, NT, 16], F32, "carg")
        sn = P.sbuf([128, NT, 16], F32, "sn")
        cs = P.sbuf([128, NT, 16], F32, "cs")
        d_p = P.dma("sp", posi[:], pos)
        d_f = P.dma("sp", invf_sb[:], invf)
        r0 = P.op("dve", lambda e: e.tensor_copy(out=posf[:], in_=posi[:]), deps=[d_p])
        r1 = P.op("dve", lambda e: e.tensor_tensor(out=ang[:], in0=posf[:].unsqueeze(2).to_broadcast([128, NT, 16]),
                                                   in1=invf_sb[:].unsqueeze(1).to_broadcast([128, NT, 16]),
                                                   op=ALU.mult), deps=[r0, d_f])
        r2 = P.op("dve", lambda e: e.tensor_scalar(out=kq[:], in0=ang[:], scalar1=1.0 / (2 * math.pi), scalar2=None,
                                                   op0=ALU.mult), deps=[r1])
        r3 = P.op("dve", lambda e: e.tensor_copy(out=kqi[:], in_=kq[:]), deps=[r2])
        r4 = P.op("dve", lambda e: e.tensor_copy(out=kq[:], in_=kqi[:]), deps=[r3])
        c1 = 6.28125
        c2 = float(np.float32(2 * math.pi - 6.28125))
        c3 = float(2 * math.pi - 6.28125 - c2)
        TWO_PI = 2 * math.pi
        r5a = P.op("dve", lambda e: e.scalar_tensor_tensor(out=red[:], in0=kq[:], scalar=-c1, in1=ang[:], op0=ALU.mult, op1=ALU.add), deps=[r4])
        r5b = P.op("dve", lambda e: e.scalar_tensor_tensor(out=red[:], in0=kq[:], scalar=-c2, in1=red[:], op0=ALU.mult, op1=ALU.add), deps=[r5a])
        r5 = P.op("dve", lambda e: e.scalar_tensor_tensor(out=red[:], in0=kq[:], scalar=-c3, in1=red[:], op0=ALU.mult, op1=ALU.add), deps=[r5b])
        w1 = P.op("dve", lambda e: e.tensor_scalar(out=carg[:], in0=red[:], scalar1=math.pi, scalar2=-TWO_PI, op0=ALU.is_gt, op1=ALU.mult), deps=[r5])
        w2 = P.op("dve", lambda e: e.tensor_tensor(out=red[:], in0=red[:], in1=carg[:], op=ALU.add), deps=[w1])
        w3 = P.op("dve", lambda e: e.tensor_scalar(out=carg[:], in0=red[:], scalar1=-math.pi, scalar2=TWO_PI, op0=ALU.is_lt, op1=ALU.mult), deps=[w2])
        r6 = P.op("dve", lambda e: e.tensor_tensor(out=sarg[:], in0=red[:], in1=carg[:], op=ALU.add), deps=[w3])
        w4 = P.op("dve", lambda e: e.tensor_scalar(out=red[:], in0=sarg[:], scalar1=math.pi / 2, scalar2=None, op0=ALU.add), deps=[r6])
        w5 = P.op("dve", lambda e: e.tensor_scalar(out=kq[:], in0=red[:], scalar1=math.pi, scalar2=-TWO_PI, op0=ALU.is_gt, op1=ALU.mult), deps=[w4])
        r7 = P.op("dve", lambda e: e.tensor_tensor(out=carg[:], in0=red[:], in1=kq[:], op=ALU.add), deps=[w5])
        lim = 3.14159
        r8 = P.op("dve", lambda e: e.tensor_scalar(out=sarg[:], in0=sarg[:], scalar1=lim, scalar2=-lim,
                                                   op0=ALU.min, op1=ALU.max), deps=[r6])
        r9 = P.op("dve", lambda e: e.tensor_scalar(out=carg[:], in0=carg[:], scalar1=lim, scalar2=-lim,
                                                   op0=ALU.min, op1=ALU.max), deps=[r7])
        rs = P.op("act", lambda e: e.activation(out=sn[:], in_=sarg[:], func=AF.Sin), deps=[r8])
        rc = P.op("act", lambda e: e.activation(out=cs[:], in_=carg[:], func=AF.Sin), deps=[r9])
        rope_ready = [rs, rc]
        ra = P.sbuf([128, 4, 16], F32, "ra")
        rb = P.sbuf([128, 4, 16], F32, "rb")

    NB = 512
    wr = Ring([P.sbuf([128, KC, NB], BF16, f"wb{i}") for i in range(2)])
    accr = Ring([P.psum([128, NB], F32, f"acc{i}") for i in range(4)])
    st32 = Ring([P.sbuf([128, NB], F32, f"st32_{i}") for i in range(3)])
    st16 = Ring([P.sbuf([128, NB], BF16, f"st16_{i}") for i in range(3)])
    if any(b[2] == "resid" for b in blocks):
        xres = Ring([P.sbuf([128, NB], F32, f"xres{i}") for i in range(3)])
    rope_last = None
    for (c0, n, kind, oname, oc0) in blocks:
        kw_, wb, rdw = wr.next()
        wl = []
        for kc0 in range(0, KC, 4):
            wl.append(P.dma("pool", wb[:, kc0:kc0 + 4, 0:n], wv[:, kc0:kc0 + 4, c0:c0 + n], deps=rdw))
        for t in range(NT):
            ka, acc, rda = accr.next()
            mm = None
            for kc in range(KC):
                mm = P.op("pe", lambda e, acc=acc, wb=wb, kc=kc, t=t, n=n: e.matmul(
                    acc[:, 0:n], lhsT=hnT[:, kc, t * 128:(t + 1) * 128], rhs=wb[:, kc, 0:n],
                    start=(kc == 0), stop=(kc == KC - 1)),
                    deps=(hn_ready + wl + rda) if kc == 0 else ())
            wr.read_by(kw_, mm)
            oap = out_aps[oname][t * 128:(t + 1) * 128, oc0:oc0 + n]
            if kind == "bf16":
                k2, sb, rd2 = st16.next()
                ev = P.op("act", lambda e, sb=sb, acc=acc, n=n: e.activation(out=sb[:, 0:n], in_=acc[:, 0:n], func=AF.Copy),
                          deps=[mm] + rd2)
                accr.read_by(ka, ev)
                od = P.dma("sp", oap, sb[:, 0:n], deps=[ev])
                st16.read_by(k2, od)
            elif kind in ("f32", "silu", "sigmoid"):
                k2, sb, rd2 = st32.next()
                fn = {"f32": AF.Copy, "silu": AF.Silu, "sigmoid": AF.Sigmoid}[kind]
                ev = P.op("act", lambda e, sb=sb, acc=acc, n=n, fn=fn: e.activation(out=sb[:, 0:n], in_=acc[:, 0:n], func=fn),
                          deps=[mm] + rd2)
                accr.read_by(ka, ev)
                od = P.dma("sp", oap, sb[:, 0:n], deps=[ev])
                st32.read_by(k2, od)
            elif kind == "resid":
                k3, xb, rd3 = xres.next()
                dx = P.dma("act", xb[:, 0:n], x[t * 128:(t + 1) * 128, c0:c0 + n], deps=rd3)
                k2, sb, rd2 = st32.next()
                ev = P.op("dve", lambda e, sb=sb, acc=acc, xb=xb, n=n: e.tensor_tensor(
                    out=sb[:, 0:n], in0=acc[:, 0:n], in1=xb[:, 0:n], op=ALU.add), deps=[mm, dx] + rd2)
                accr.read_by(ka, ev)
                xres.read_by(k3, ev)
                od = P.dma("sp", oap, sb[:, 0:n], deps=[ev])
                st32.read_by(k2, od)
            elif kind == "rope":
                assert n % 128 == 0
                nh = n // 128
                k2, s32, rd2 = st32.next()
                ev = P.op("act", lambda e, s32=s32, acc=acc, n=n: e.activation(out=s32[:, 0:n], in_=acc[:, 0:n], func=AF.Copy),
                          deps=[mm] + rd2)
                accr.read_by(ka, ev)
                k3, sb, rd3 = st16.next()
                cp = P.op("dve", lambda e, sb=sb, s32=s32, n=n: e.tensor_copy(out=sb[:, 0:n], in_=s32[:, 0:n]),
                          deps=[ev] + rd3)
                s3 = s32[:, 0:n].rearrange("p (h d) -> p h d", d=128)
                o3 = sb[:, 0:n].rearrange("p (h d) -> p h d", d=128)
                t1 = s3[:, :, 0:16]
                t2 = s3[:, :, 16:32]
                cb = cs[:, t, :].unsqueeze(1).to_broadcast([128, nh, 16])
                sbb = sn[:, t, :].unsqueeze(1).to_broadcast([128, nh, 16])
                A = ra[:, 0:nh, :]
                B = rb[:, 0:nh, :]
                q1 = P.op("dve", lambda e, A=A, t1=t1, cb=cb: e.tensor_tensor(out=A, in0=t1, in1=cb, op=ALU.mult),
                          deps=[ev, rope_last] + rope_ready)
                q2 = P.op("dve", lambda e, B=B, t2=t2, sbb=sbb: e.tensor_tensor(out=B, in0=t2, in1=sbb, op=ALU.mult), deps=[q1])
                q3 = P.op("dve", lambda e, A=A, B=B, o3=o3: e.tensor_tensor(out=o3[:, :, 0:16], in0=A, in1=B, op=ALU.subtract),
                          deps=[q2, cp])
                q4 = P.op("dve", lambda e, A=A, t2=t2, cb=cb: e.tensor_tensor(out=A, in0=t2, in1=cb, op=ALU.mult), deps=[q3])
                q5 = P.op("dve", lambda e, B=B, t1=t1, sbb=sbb: e.tensor_tensor(out=B, in0=t1, in1=sbb, op=ALU.mult), deps=[q4])
                q6 = P.op("dve", lambda e, A=A, B=B, o3=o3: e.tensor_tensor(out=o3[:, :, 16:32], in0=A, in1=B, op=ALU.add),
                          deps=[q5])
                rope_last = q6
                st32.read_by(k2, q6, cp)
                od = P.dma("sp", oap, sb[:, 0:n], deps=[q6])
                st16.read_by(k3, od)
            else:
                raise ValueError(kind)
            out_dmas.append(od)
    P.wait_all("sp", out_dmas)
    P.emit()
    return nc


NSA_SCALE = 128 ** -0.5
BIGV = 1e30


def build_nsa_core(nqb=64):
    nc = bass.Bass("TRN2", target_bir_lowering=False)
    P = Prog(nc)
    S = SEQ

    def din(name, shape, dt):
        return nc.dram_tensor(name, list(shape), dt, kind="ExternalInput").ap()

    qT_d = din("qT", [128, 4, S], BF16)
    kcT_d = din("kcT", [128, S], BF16)
    vcT_d = din("vcT", [128, S], BF16)
    ksT_d = din("ksT", [128, S], BF16)
    kwT_d = din("kwT", [128, S], BF16)
    vs_d = din("vs", [128, 64, 128], BF16)
    vw_d = din("vw", [128, 64, 128], BF16)
    gate_d = din("gate", [128, 64, 12], F32)
    sz_d = din("sz", [128, 64, 512], F32)
    w1k_d = din("w1k", [128, 32, 128], F32)
    w1v_d = din("w1v", [128, 32, 128], F32)
    posk_d = din("posk", [128, 32], F32)
    posv_d = din("posv", [128, 32], F32)
    w2k_d = din("w2k", [128, 128], F32)
    w2v_d = din("w2v", [128, 128], F32)
    E_d = din("E", [128, S], BF16)
    ov_d = din("ov", [128, 4, 128], BF16)
    idb_d = din("identb", [128, 128], BF16)
    o_d = nc.dram_tensor("o", [S, 512], BF16, kind="ExternalOutput").ap()

    ksT = P.sbuf([128, S], BF16, "ksT")
    kwT = P.sbuf([128, S], BF16, "kwT")
    kcT = P.sbuf([128, S], BF16, "kcT")
    vcT = P.sbuf([128, S], BF16, "vcT")
    Emat = P.sbuf([128, S], BF16, "Emat")
    vs_e = P.sbuf([128, 64, 130], BF16, "vs_e")
    vw_e = P.sbuf([128, 64, 130], BF16, "vw_e")
    gates = P.sbuf([128, 64, 12], F32, "gates")
    w1k = P.sbuf([128, 32, 128], BF16, "w1k")
    w1v = P.sbuf([128, 32, 128], BF16, "w1v")
    posk = P.sbuf([128, 32], BF16, "posk")
    posv = P.sbuf([128, 32], BF16, "posv")
    w2k = P.sbuf([128, 128], BF16, "w2k")
    w2v = P.sbuf([128, 128], BF16, "w2v")
    identb = P.sbuf([128, 128], BF16, "identb")
    CV = P.sbuf([128, 4, 258], BF16, "CV")
    kcmpT = P.sbuf([128, 512], BF16, "kcmpT")
    hid_k = P.sbuf([128, 512], BF16, "hid_k")
    hid_v = P.sbuf([128, 512], BF16, "hid_v")
    cb = P.sbuf([128, 2], F32, "cb")

    banks = [P.psum([128, 512], F32, f"bank{i}") for i in range(7)]
    tbank = P.psum([128, 1024], BF16, "tbank")
    S_banks = banks[0:2]
    accA, accB, impb = banks[2], banks[3], banks[4]
    M_banks = banks[5:7]

    ld = {}
    ld["kcT"] = P.dma("sp", kcT[:], kcT_d)
    ld["vcT"] = P.dma("sp", vcT[:], vcT_d)
    for nm, sb, dd in (("w1k", w1k, w1k_d), ("w1v", w1v, w1v_d)):
        ld[nm] = [P.dma("pool", sb[:, l0:l0 + 8, :], dd[:, l0:l0 + 8, :]) for l0 in range(0, 32, 8)]
    ld["posk"] = P.dma("pool", posk[:], posk_d)
    ld["posv"] = P.dma("pool", posv[:], posv_d)
    ld["w2k"] = P.dma("pool", w2k[:], w2k_d)
    ld["w2v"] = P.dma("pool", w2v[:], w2v_d)
    ld["ksT"] = P.dma("act", ksT[:], ksT_d)
    ld["kwT"] = P.dma("act", kwT[:], kwT_d)
    ld["E"] = P.dma("act", Emat[:], E_d)
    ld["vs"] = P.dma("sp", vs_e[:, :, 0:128], vs_d)
    ld["vw"] = P.dma("sp", vw_e[:, :, 0:128], vw_d)
    ld["gates"] = P.dma("sp", gates[:], gate_d)
    ld["ov"] = P.dma("sp", CV[:, :, 0:128], ov_d)
    ld["idb"] = P.dma("sp", identb[:], idb_d)
    ms = [P.op("pool", lambda e: e.memset(hid_k[:, 511:512], 0.0)),
          P.op("pool", lambda e: e.memset(hid_v[:, 511:512], 0.0)),
          P.op("pool", lambda e: e.memset(kcmpT[:, 511:512], 0.0)),
          P.op("pool", lambda e: e.memset(vs_e[:, :, 128:130], 1.0), deps=[ld["vs"]]),
          P.op("pool", lambda e: e.memset(vw_e[:, :, 128:130], 1.0), deps=[ld["vw"]])]
    ones_ready = ms[-1]

    def compress(xT, w1, pos, hid, col, d_x, d_w1, d_pos):
        hp = banks[0][:, 0:511]
        mm = None
        for l in range(32):
            mm = P.op("pe", lambda e, l=l: e.matmul(hp, lhsT=w1[:, l, :], rhs=xT[:, l:l + 8161:16],
                                                    start=(l == 0), stop=(l == 31)), deps=[d_x] + d_w1)
        cp = banks[1][:, 0:1]
        mc = None
        for l in range(32):
            mc = P.op("pe", lambda e, l=l: e.matmul(cp, lhsT=w1[:, l, :], rhs=pos[:, l:l + 1],
                                                    start=(l == 0), stop=(l == 31)), deps=[d_pos] + d_w1)
        c1 = P.op("dve", lambda e: e.tensor_copy(out=cb[:, col:col + 1], in_=cp), deps=[mc])
        h = P.op("act", lambda e: e.activation(out=hid[:, 0:511], in_=hp, func=AF.Silu, bias=cb[:, col:col + 1]),
                 deps=[mm, c1])
        return h, c1

    hk, c1k = compress(kcT, w1k, posk, hid_k, 0, ld["kcT"], ld["w1k"], ld["posk"])
    kp = banks[2][:, 0:511]
    mk = P.op("pe", lambda e: e.matmul(kp, lhsT=w2k[:], rhs=hid_k[:, 0:511], start=True, stop=True), deps=[hk, ld["w2k"]])
    kev = P.op("dve", lambda e: e.tensor_copy(out=kcmpT[:, 0:511], in_=kp), deps=[mk])
    hv_dep = [hk, c1k]
    hp_guard = P.op("pe", None, hv_dep)
    hv, c1v = compress(vcT, w1v, posv, hid_v, 1, ld["vcT"], ld["w1v"], ld["posv"])
    vev = []
    for j in range(4):
        vp = banks[3 + (j % 2)][:, 0:128]
        mv = P.op("pe", lambda e, vp=vp, j=j: e.matmul(vp, lhsT=hid_v[:, j * 128:(j + 1) * 128], rhs=w2v[:],
                                                       start=True, stop=True), deps=[hv, ld["w2v"]] + vev[-2:-1])
        vev.append(P.op("dve", lambda e, vp=vp, j=j: e.tensor_copy(out=CV[:, j, 128:256], in_=vp), deps=[mv]))
    cv_ones = P.op("dve", lambda e: e.memset(CV[:, :, 256:258], 1.0), deps=[vev[-1], vev[-2], ld["ov"]])
    cmp_ready = [kev, vev[-1], vev[-2], ld["ov"], ones_ready, cv_ones]

    qbr = Ring([P.sbuf([128, 4, 128], BF16, f"qb{i}") for i in range(3)])
    Pr = Ring([P.sbuf([128, 4, 128], BF16, f"Pt{i}") for i in range(3)])
    Sr = Ring(S_banks)
    Mr = Ring(M_banks)
    szr = Ring([P.sbuf([128, 512], F32, f"szb{i}") for i in range(2)])
    omr = Ring([P.sbuf([128, 4, 128], F32, f"om{i}") for i in range(2)])
    outr = Ring([P.sbuf([128, 512], BF16, f"o16_{i}") for i in range(2)])
    selTr = Ring([P.sbuf([128, 128], BF16, f"selT{i}") for i in range(2)])
    sc = P.sbuf([128, 128], F32, "score")
    sc2 = P.sbuf([128, 128], F32, "score2")
    m8a = P.sbuf([128, 8], F32, "m8a")
    m8b = P.sbuf([128, 8], F32, "m8b")
    selm = P.sbuf([128, 128], BF16, "selm")
    lt = P.sbuf([128, 4], F32, "lt")
    coef = P.sbuf([128, 4], F32, "coef")

    state = {"acc_readers": list(cmp_ready), "out_dmas": [], "dve_last": None}

    def acc_view(r, n):
        if r < 3:
            return accA[:, r * 129:r * 129 + n]
        return accB[:, 0:n]

    def run_tiles(qb, d_q, tiles, KT, d_K, Vt, d_V, kind, selT_ap=None, selT_ready=None):
        last_pv = None
        nt = len(tiles)
        st1 = {}

        def stage1(ti):
            j, mask = tiles[ti]
            ks_, Sb, rdS = Sr.next()
            mmS = P.op("pe", lambda e, Sb=Sb, j=j: e.matmul(Sb[:], lhsT=KT[:, j * 128:(j + 1) * 128],
                                                             rhs=qb[:].rearrange("p r q -> p (r q)"),
                                                             start=True, stop=True), deps=[d_q] + d_K + rdS)
            mmM = None
            km_ = Mb = None
            if kind == "sel":
                km_, Mb, rdM = Mr.next()
                mmM = P.op("pe", lambda e, Mb=Mb, j=j: e.matmul(Mb[:, 0:128], lhsT=Emat[:, j * 128:(j + 1) * 128],
                                                                 rhs=selT_ap, start=True, stop=True),
                           deps=[ld["E"], selT_ready] + rdM)
            st1[ti] = (ks_, Sb, mmS, km_, Mb, mmM)

        stage1(0)
        for ti, (j, mask) in enumerate(tiles):
            if ti + 1 < nt:
                stage1(ti + 1)
            ks_, Sb, mmS, km_, Mb, mmM = st1.pop(ti)
            kp_, Pt, rdP = Pr.next()
            Pf = Pt[:].rearrange("p r q -> p (r q)")
            ex = P.op("act", lambda e, Pf=Pf, Sb=Sb: e.activation(out=Pf, in_=Sb[:], func=AF.Exp, scale=NSA_SCALE),
                      deps=[mmS] + rdP)
            Sr.read_by(ks_, ex)
            lastm = ex
            if kind == "sel":
                mu = P.op("dve", lambda e, Pt=Pt, Mb=Mb: e.tensor_tensor(
                    out=Pt[:], in0=Pt[:], in1=Mb[:, 0:128].unsqueeze(1).to_broadcast([128, 4, 128]), op=ALU.mult),
                    deps=[ex, mmM])
                Mr.read_by(km_, mu)
                lastm = mu
            if mask is not None:
                base, cm, qstep, cop = mask
                lastm = P.op("pool", lambda e, Pt=Pt, base=base, cm=cm, qstep=qstep, cop=cop: e.affine_select(
                    out=Pt[:], in_=Pt[:], pattern=[[0, 4], [qstep, 128]], compare_op=cop, fill=0.0,
                    base=base, channel_multiplier=cm), deps=[lastm])
            first = (ti == 0)
            lastt = (ti == nt - 1)
            for r in range(4):
                rhs = (Vt[:, j, 128:257] if kind == "cmp" else Vt[:, j, 0:129])
                last_pv = P.op("pe", lambda e, r=r, Pt=Pt, rhs=rhs, first=first, lastt=lastt: e.matmul(
                    acc_view(r, 129), lhsT=Pt[:, r, :], rhs=rhs, start=(first and r in (0, 3)), stop=lastt),
                    deps=[lastm] + d_V + (state["acc_readers"] if first else []))
            if kind == "cmp":
                for r in range(4):
                    last_pv = P.op("pe", lambda e, r=r, Pt=Pt, j=j, first=first, lastt=lastt: e.matmul(
                        impb[:, r * 128:(r + 1) * 128], lhsT=Pt[:, r, :], rhs=Vt[:, j, 0:128], start=(first and r == 0), stop=lastt),
                        deps=[lastm])
            Pr.read_by(kp_, last_pv)
        return last_pv

    def evac_branch(last_pv, om, br, i, first_branch, extra_dep):
        ops = []
        d0 = P.op("dve", lambda e: e.tensor_copy(out=lt[:, 0:3], in_=accA[:, 128:387:129]), deps=[last_pv, state["dve_last"]] + extra_dep)
        d1 = P.op("dve", lambda e: e.tensor_copy(out=lt[:, 3:4], in_=accB[:, 128:129]), deps=[last_pv])
        d2 = P.op("dve", lambda e: e.tensor_scalar(out=lt[:], in0=lt[:], scalar1=1e-30, scalar2=None, op0=ALU.max), deps=[d0, d1])
        d3 = P.op("dve", lambda e: e.reciprocal(out=lt[:], in_=lt[:]), deps=[d2])
        d4 = P.op("dve", lambda e: e.tensor_tensor(out=coef[:], in0=lt[:], in1=gates[:, i, br * 4:(br + 1) * 4], op=ALU.mult),
                  deps=[d3, ld["gates"]])
        lastd = d4
        for r in range(4):
            if first_branch:
                lastd = P.op("dve", lambda e, r=r: e.tensor_scalar(out=om[:, r, :], in0=acc_view(r, 128), scalar1=coef[:, r:r + 1],
                                                                    scalar2=None, op0=ALU.mult), deps=[d4])
            else:
                lastd = P.op("dve", lambda e, r=r: e.scalar_tensor_tensor(out=om[:, r, :], in0=acc_view(r, 128),
                                                                           scalar=coef[:, r:r + 1], in1=om[:, r, :],
                                                                           op0=ALU.mult, op1=ALU.add), deps=[d4, lastd])
            ops.append(lastd)
        state["dve_last"] = lastd
        return ops, d3

    def issue_loads(i):
        kq_, qb, rdq = qbr.next()
        d_q = P.dma("sp", qb[:], qT_d[:, :, 128 * i:128 * i + 128], deps=rdq)
        kz_, szb, rdz = szr.next()
        d_sz = P.dma("sp", szb[:], sz_d[:, i, :], deps=rdz)
        return (kq_, qb, d_q, kz_, szb, d_sz)

    nxt_loads = issue_loads(0)
    for i in range(nqb):
        s0 = 128 * i
        kq_, qb, d_q, kz_, szb, d_sz = nxt_loads
        ko_, om, rdo = omr.next()

        tiles = []
        for j in range(4):
            base = s0 - 2048 * j - 31
            if base + 127 < 0:
                continue
            mask = None if base - 16 * 127 >= 0 else (base, -16, 1, ALU.is_ge)
            tiles.append((j, mask))
        lp = run_tiles(qb, d_q, tiles, kcmpT, cmp_ready, CV, cmp_ready, "cmp")
        ev_ops, d3 = evac_branch(lp, om, 0, i, True, rdo)
        lastd = None
        for r in range(4):
            if r == 0:
                lastd = P.op("dve", lambda e: e.tensor_scalar(out=sc[:], in0=impb[:, 0:128], scalar1=lt[:, 0:1], scalar2=None,
                                                              op0=ALU.mult), deps=[d3, lp])
            else:
                lastd = P.op("dve", lambda e, r=r: e.scalar_tensor_tensor(out=sc[:], in0=impb[:, r * 128:(r + 1) * 128],
                                                                           scalar=lt[:, r:r + 1], in1=sc[:], op0=ALU.mult,
                                                                           op1=ALU.add), deps=[lastd])
        imp_done = lastd
        sets = [(slice(0, 128), slice(0, 1), BIGV), (slice(0, 128), slice(2 * i, 2 * i + 1), BIGV),
                (slice(64, 128), slice(2 * i + 1, 2 * i + 2), BIGV), (slice(0, 64), slice(2 * i + 1, 128), -BIGV)]
        if i >= 1:
            sets.append((slice(0, 64), slice(2 * i - 1, 2 * i), BIGV))
        if 2 * i + 2 < 128:
            sets.append((slice(64, 128), slice(2 * i + 2, 128), -BIGV))
        for (ps_, cs_, val) in sets:
            lastd = P.op("dve", lambda e, ps_=ps_, cs_=cs_, val=val: e.memset(sc[ps_, cs_], val), deps=[lastd])
        t1 = P.op("dve", lambda e: e.max(out=m8a[:], in_=sc[:]), deps=[lastd])
        t2 = P.op("dve", lambda e: e.match_replace(out=sc2[:], in_to_replace=m8a[:], in_values=sc[:], imm_value=-3e30), deps=[t1])
        t3 = P.op("dve", lambda e: e.max(out=m8b[:], in_=sc2[:]), deps=[t2])
        t4 = P.op("dve", lambda e: e.tensor_scalar(out=selm[:], in0=sc[:], scalar1=m8b[:, 7:8], scalar2=None, op0=ALU.is_ge),
                  deps=[t3, state.get("selm_reader")])
        state["dve_last"] = t4
        state["acc_readers"] = ev_ops + [imp_done]

        tiles = []
        for j in range(max(0, i - 4), i + 1):
            if j == i - 4:
                mask = (0, 1, -1, ALU.is_gt)
            elif j == i:
                mask = (0, -1, 1, ALU.is_ge)
            else:
                mask = None
            tiles.append((j, mask))
        lp = run_tiles(qb, d_q, tiles, kwT, [ld["kwT"]], vw_e, [ld["vw"], ones_ready], "win")
        ev_ops, _ = evac_branch(lp, om, 2, i, False, [])
        state["acc_readers"] = ev_ops

        tr = P.op("pe", lambda e: e.transpose(out=tbank[:, 0:128], in_=selm[:], identity=identb[:]),
                  deps=[t4, ld["idb"], state.get("tbank_reader")])
        state["selm_reader"] = tr
        kt_, selT, rdt = selTr.next()
        cpT = P.op("dve", lambda e, selT=selT: e.tensor_copy(out=selT[:], in_=tbank[:, 0:128]), deps=[tr] + rdt)
        state["tbank_reader"] = cpT
        state["dve_last"] = cpT

        tiles = [(j, (0, -1, 1, ALU.is_ge) if j == i else None) for j in range(0, i + 1)]
        lp = run_tiles(qb, d_q, tiles, ksT, [ld["ksT"]], vs_e, [ld["vs"], ones_ready], "sel", selT_ap=selT[:], selT_ready=cpT)
        selTr.read_by(kt_, lp)
        qbr.read_by(kq_, lp)
        ev_ops, _ = evac_branch(lp, om, 1, i, False, [])
        state["acc_readers"] = ev_ops

        kout_, o16, rdout = outr.next()
        fin = P.op("dve", lambda e, o16=o16, om=om, szb=szb: e.tensor_tensor(
            out=o16[:], in0=om[:].rearrange("p r d -> p (r d)"), in1=szb[:], op=ALU.mult), deps=ev_ops + [d_sz] + rdout)
        state["dve_last"] = fin
        szr.read_by(kz_, fin)
        omr.read_by(ko_, fin)
        if i + 1 < nqb:
            nxt_loads = issue_loads(i + 1)
        od = P.dma("sp", o_d[s0:s0 + 128, :], o16[:], deps=[fin])
        outr.read_by(kout_, od)
        state["out_dmas"].append(od)
    P.wait_all("sp", state["out_dmas"])
    P.emit()
    return nc


def build_hgrn_core(layer_j, nst=16):
    nc = bass.Bass("TRN2", target_bir_lowering=False)
    P = Prog(nc)
    S = SEQ

    def din(name, shape, dt):
        return nc.dram_tensor(name, list(shape), dt, kind="ExternalInput").ap()

    qT_d = din("qT", [128, 4, S], F32)
    fT_d = din("fT", [128, 4, S], F32)
    szT_d = din("szT", [128, 4, S], F32)
    iv_d = din("iv", [128, 64, 4, 128], BF16)
    lbl_d = din("lbl", [128, 4, 2], F32)
    gw_d = din("gw", [128, 1], F32)
    mbd_d = din("maskbd", [128, 128], F32)
    cm_d = din("cm", [128, 4], F32)
    seg_d = din("seg", [128, 2048], F32)
    ones_d = din("ones", [128, 128], F32)
    idb_d = din("identb", [128, 128], BF16)
    oT_d = nc.dram_tensor("oT", [128, 4, S], BF16, kind="ExternalOutput").ap()

    lbl = P.sbuf([128, 4, 2], F32, "lbl")
    gw = P.sbuf([128, 1], F32, "gw")
    mbd = P.sbuf([128, 128], F32, "mbd")
    cm = P.sbuf([128, 4], F32, "cm")
    seg = P.sbuf([128, 2048], F32, "seg")
    ones = P.sbuf([128, 128], F32, "ones")
    identb = P.sbuf([128, 128], BF16, "identb")
    lb = P.sbuf([128, 4], F32, "lb")
    oml = P.sbuf([128, 4], F32, "oml")
    ld = {}
    for nm, sb, dd in (("lbl", lbl, lbl_d), ("gw", gw, gw_d), ("mbd", mbd, mbd_d), ("cm", cm, cm_d), ("seg", seg, seg_d),
                       ("ones", ones, ones_d), ("idb", identb, idb_d)):
        ld[nm] = P.dma("act", sb[:], dd)
    if layer_j >= 1:
        assert layer_j == 1
        a0 = P.op("dve", lambda e: e.tensor_tensor(out=lb[:], in0=lbl[:, :, 0], in1=lbl[:, :, 1], op=ALU.subtract), deps=[ld["lbl"]])
        a1 = P.op("act", lambda e: e.activation(out=lb[:], in_=lb[:], func=AF.Exp), deps=[a0])
        a2 = P.op("dve", lambda e: e.tensor_scalar(out=lb[:], in0=lb[:], scalar1=1.0, scalar2=None, op0=ALU.add), deps=[a1])
        a3 = P.op("dve", lambda e: e.reciprocal(out=lb[:], in_=lb[:]), deps=[a2])
        lb_ready = P.op("dve", lambda e: e.tensor_scalar(out=oml[:], in0=lb[:], scalar1=-1.0, scalar2=1.0, op0=ALU.mult, op1=ALU.add),
                        deps=[a3])

    N = 2048
    Ar = Ring([P.sbuf([128, 4, 512], F32, f"A{i}") for i in range(2)])
    qtr = Ring([P.sbuf([128, 4, 512], F32, f"qt{i}") for i in range(2)])
    szr = Ring([P.sbuf([128, 4, 512], F32, f"szt{i}") for i in range(2)])
    ebr = Ring([P.sbuf([128, 4, 512], F32, f"eb{i}") for i in range(2)])
    qbr = Ring([P.sbuf([128, 4, 512], BF16, f"qb{i}") for i in range(2)])
    kbr = Ring([P.sbuf([128, 4, 512], BF16, f"kb{i}") for i in range(2)])
    Bt = P.sbuf([128, 4, 512], F32, "Bt")
    Ct = P.sbuf([128, 4, 512], F32, "Ct")
    Dn = P.sbuf([128, 4, 512], F32, "Dn")
    ivr = Ring([P.sbuf([128, 4, 128], BF16, f"ivt{i}") for i in range(3)])
    kbmr = Ring([P.sbuf([128, 4, 128], BF16, f"kbm{i}") for i in range(8)])
    attr = Ring([P.sbuf([128, 128], BF16, f"attT{i}") for i in range(3)])
    G = [[P.sbuf([128, 128], F32, f"G{h}_{k}") for k in range(2)] for h in range(4)]
    sbf = [Ring([P.sbuf([128, 128], BF16, f"sbf{h}_{k}") for k in range(2)]) for h in range(4)]
    o_sb = P.sbuf([128, 512], F32, "o_sb")
    osq = P.sbuf([128, 512], F32, "osq")
    rst = P.sbuf([128, 512], F32, "rst")
    o16r = Ring([P.sbuf([128, 4, 128], BF16, f"o16_{i}") for i in range(2)])

    tb = P.psum([128, 1024], BF16, "tb")
    att_r = Ring([P.psum([128, 512], F32, f"attp{i}") for i in range(2)])
    oT_r = Ring([P.psum([128, 512], F32, f"oTp{i}") for i in range(2)])
    U_r = Ring([P.psum([128, 512], F32, f"Up{i}") for i in range(2)])
    ss_ps = P.psum([128, 512], F32, "ssp")

    fl = lambda t: t[:].rearrange("p h n -> p (h n)")
    st = {"dve_last": None, "act_last": None, "Gprod": [None] * 4, "Gidx": [0] * 4, "sbf_act": [None] * 4,
          "prev_ebl": [None] * 4, "prev_eb_op": None, "tb_reader": None, "ss_reader": None, "fin_last": None,
          "o_sb_readers": [], "out_dmas": [], "BtCt_readers": []}
    chunk_global = 0
    for T in range(nst):
        t0 = T * 512
        ka_, A, rdA = Ar.next()
        d_f = P.dma("sp", A[:], fT_d[:, :, t0:t0 + 512], deps=rdA)
        kq_, qt, rdq = qtr.next()
        d_q = P.dma("sp", qt[:], qT_d[:, :, t0:t0 + 512], deps=rdq)
        kz_, szt, rdz = szr.next()
        d_z = P.dma("sp", szt[:], szT_d[:, :, t0:t0 + 512], deps=rdz)
        e1 = P.op("act", lambda e, A=A: e.activation(out=fl(A), in_=fl(A), func=AF.Exp, scale=-1.0), deps=[d_f])
        v1 = P.op("dve", lambda e, A=A: e.tensor_scalar(out=fl(Bt), in0=fl(A), scalar1=1.0, scalar2=None, op0=ALU.add),
                  deps=[e1] + st["BtCt_readers"])
        v2 = P.op("dve", lambda e: e.reciprocal(out=fl(Bt), in_=fl(Bt)), deps=[v1])
        v3 = P.op("dve", lambda e, A=A: e.tensor_tensor(out=fl(A), in0=fl(A), in1=fl(Bt), op=ALU.mult), deps=[v2])
        lastv = v3
        if layer_j >= 1:
            for h in range(4):
                lastv = P.op("dve", lambda e, A=A, h=h: e.tensor_scalar(out=A[:, h, :], in0=A[:, h, :], scalar1=oml[:, h:h + 1],
                                                                        scalar2=None, op0=ALU.mult), deps=[lastv, lb_ready])
            for h in range(4):
                lastv = P.op("dve", lambda e, h=h: e.tensor_scalar(out=Bt[:, h, :], in0=Bt[:, h, :], scalar1=oml[:, h:h + 1],
                                                                   scalar2=lb[:, h:h + 1], op0=ALU.mult, op1=ALU.add), deps=[lastv])
        e2 = P.op("act", lambda e: e.activation(out=fl(Ct), in_=fl(Bt), func=AF.Ln), deps=[lastv])
        v4 = P.op("dve", lambda e: e.tensor_tensor_scan(out=fl(Bt), data0=seg[:], data1=fl(Ct), initial=0.0,
                                                        op0=ALU.mult, op1=ALU.add), deps=[e2, ld["seg"]])
        ke_, eb, rde = ebr.next()
        e3 = P.op("act", lambda e, eb=eb: e.activation(out=fl(eb), in_=fl(Bt), func=AF.Exp), deps=[v4] + rde)
        e4 = P.op("act", lambda e: e.activation(out=fl(Dn), in_=fl(Bt), func=AF.Exp, scale=-1.0), deps=[v4, st["dve_last"]])
        kqb_, qb, rdqb = qbr.next()
        v5 = P.op("dve", lambda e, qb=qb, qt=qt, eb=eb: e.tensor_tensor(out=fl(qb), in0=fl(qt), in1=fl(eb), op=ALU.mult),
                  deps=[e3, d_q] + rdqb)
        kkb_, kb, rdkb = kbr.next()
        v6 = P.op("dve", lambda e, kb=kb, A=A: e.tensor_tensor(out=fl(kb), in0=fl(A), in1=fl(Dn), op=ALU.mult),
                  deps=[e4, lastv] + rdkb)
        st["dve_last"] = v6
        st["BtCt_readers"] = [e3, e4, v4]
        Ar.read_by(ka_, v6)
        qtr.read_by(kq_, v5)
        qk_ready = [v5, v6]

        for tt in range(4):
            o = tt * 128
            J = T * 4 + tt
            kiv_, ivt, rdiv = ivr.next()
            d_iv = P.dma("sp", ivt[:], iv_d[:, J, :, :], deps=rdiv)
            ko_, oTp, rdo = oT_r.next()
            kbms = []
            last_pe = None
            for h in range(4):
                trp = P.op("pe", lambda e, kb=kb, h=h, o=o: e.transpose(out=tb[:, 0:128], in_=kb[:, h, o:o + 128], identity=identb[:]),
                           deps=qk_ready + [ld["idb"], st["tb_reader"]])
                kk_, kbm, rdk = kbmr.next()
                lastk = None
                for c in range(4):
                    lastk = P.op("act", lambda e, kbm=kbm, c=c: e.activation(out=kbm[:, c, :], in_=tb[:, 0:128], func=AF.Copy,
                                                                              scale=cm[:, c:c + 1]), deps=[trp, ld["cm"]] + rdk)
                st["tb_reader"] = lastk
                kbms.append((kk_, kbm, lastk))
                kat_, attp, rdat = att_r.next()
                ma = P.op("pe", lambda e, attp=attp, kb=kb, qb=qb, h=h, o=o: e.matmul(
                    attp[:, 0:128], lhsT=kb[:, h, o:o + 128], rhs=qb[:, h, o:o + 128], start=True, stop=True),
                    deps=qk_ready + rdat)
                ks_, attT, rds = attr.next()
                va = P.op("dve", lambda e, attT=attT, attp=attp: e.tensor_tensor(out=attT[:], in0=attp[:, 0:128], in1=mbd[:], op=ALU.mult),
                          deps=[ma, ld["mbd"]] + rds)
                att_r.read_by(kat_, va)
                mi = P.op("pe", lambda e, oTp=oTp, ivt=ivt, attT=attT, h=h: e.matmul(
                    oTp[:, h * 128:(h + 1) * 128], lhsT=ivt[:, h, :], rhs=attT[:], start=(h == 0), stop=False),
                    deps=[va, d_iv] + rdo)
                attr.read_by(ks_, mi)
                last_pe = mi
            for c in range(4):
                for h in range(4):
                    first_chunk = (chunk_global == 0)
                    gi = st["Gidx"][h]
                    Gold, Gnew = G[h][gi], G[h][1 - gi]
                    if not first_chunk:
                        ebl, ebop = st["prev_ebl"][h]
                        ksb_, sb_, rdsb = sbf[h].next()
                        sa = P.op("act", lambda e, sb_=sb_, Gold=Gold, ebl=ebl: e.activation(out=sb_[:], in_=Gold[:], func=AF.Copy, scale=ebl),
                                  deps=[st["Gprod"][h], ebop] + rdsb)
                        mint = P.op("pe", lambda e, oTp=oTp, sb_=sb_, qb=qb, h=h, o=o, c=c: e.matmul(
                            oTp[:, h * 128 + 32 * c:h * 128 + 32 * c + 32], lhsT=sb_[:], rhs=qb[:, h, o + 32 * c:o + 32 * c + 32],
                            start=False, stop=(c == 3)), deps=[sa])
                        sbf[h].read_by(ksb_, mint)
                        last_pe = mint
                    ku_, Up, rdu = U_r.next()
                    kk_, kbm, lastk = kbms[h]
                    mu = P.op("pe", lambda e, Up=Up, kbm=kbm, ivt=ivt, h=h, c=c: e.matmul(
                        Up[:, 0:128], lhsT=kbm[:, c, :], rhs=ivt[:, h, :], start=True, stop=True), deps=[lastk, d_iv] + rdu)
                    last_pe = mu
                    if first_chunk:
                        gp = P.op("dve", lambda e, Gnew=Gnew, Up=Up: e.tensor_copy(out=Gnew[:], in_=Up[:, 0:128]), deps=[mu])
                    else:
                        gp = P.op("dve", lambda e, Gnew=Gnew, Gold=Gold, Up=Up, ebl=ebl: e.scalar_tensor_tensor(
                            out=Gnew[:], in0=Gold[:], scalar=ebl, in1=Up[:, 0:128], op0=ALU.mult, op1=ALU.add),
                            deps=[mu, st["Gprod"][h], ebop, st["sbf_act"][h]])
                        st["sbf_act"][h] = sa
                    U_r.read_by(ku_, gp)
                    st["Gprod"][h] = gp
                    st["Gidx"][h] = 1 - gi
                    col = o + 32 * c + 31
                    st["prev_ebl"][h] = (eb[:, h, col:col + 1], e3)
                    if h == 3:
                        chunk_global += 1
                for h in range(4):
                    if c == 3:
                        kbmr.read_by(kbms[h][0], last_pe)
            ivr.read_by(kiv_, last_pe)
            f0 = P.op("dve", lambda e, oTp=oTp: e.tensor_copy(out=o_sb[:], in_=oTp[:]), deps=[last_pe] + st["o_sb_readers"])
            oT_r.read_by(ko_, f0)
            f1 = P.op("act", lambda e: e.activation(out=osq[:], in_=o_sb[:], func=AF.Square), deps=[f0, st["fin_last"]])
            f2 = P.op("pe", lambda e: e.matmul(ss_ps[:], lhsT=ones[:], rhs=osq[:], start=True, stop=True),
                      deps=[f1, ld["ones"], st["ss_reader"]])
            f3 = P.op("dve", lambda e: e.tensor_scalar(out=rst[:], in0=ss_ps[:], scalar1=1.0 / 128, scalar2=RMS_EPS, op0=ALU.mult, op1=ALU.add),
                      deps=[f2, st["fin_last"]])
            st["ss_reader"] = f3
            f4 = P.op("act", lambda e: e.activation(out=rst[:], in_=rst[:], func=AF.Ln), deps=[f3])
            f5 = P.op("act", lambda e: e.activation(out=rst[:], in_=rst[:], func=AF.Exp, scale=-0.5), deps=[f4])
            f6 = P.op("dve", lambda e: e.tensor_tensor(out=o_sb[:], in0=o_sb[:], in1=rst[:], op=ALU.mult), deps=[f5, f1])
            k16_, o16, rd16 = o16r.next()
            f7 = P.op("dve", lambda e, o16=o16, szt=szt, o=o: e.scalar_tensor_tensor(
                out=o16[:], in0=o_sb[:].rearrange("p (h n) -> p h n", h=4), scalar=gw[:, 0:1], in1=szt[:, :, o:o + 128],
                op0=ALU.mult, op1=ALU.mult), deps=[f6, d_z, ld["gw"]] + rd16)
            st["fin_last"] = f7
            st["o_sb_readers"] = [f7]
            od = P.dma("pool", oT_d[:, :, t0 + o:t0 + o + 128], o16[:], deps=[f7])
            o16r.read_by(k16_, od)
            st["out_dmas"].append(od)
        szr.read_by(kz_, st["fin_last"])
        qbr.read_by(kqb_, last_pe)
        kbr.read_by(kkb_, last_pe)
        ebr.read_by(ke_, st["Gprod"][3], st["sbf_act"][3] if st["sbf_act"][3] is not None else st["Gprod"][3])
    P.wait_all("pool", st["out_dmas"])
    P.emit()
    return nc


def _blocks_cols(c0, ncols, kind, oname, oc0, nb=512):
    out = []
    o = 0
    while o < ncols:
        n = min(nb, ncols - o)
        out.append((c0 + o, n, kind, oname, oc0 + o))
        o += n
    return out


_CACHE = {}


def _get(key, fn):
    return fn()


def _inv_freq():
    half = 16
    return (np.float32(ROPE_THETA) ** (-np.arange(half, dtype=np.float32) / np.float32(half))).astype(np.float32)


def run_nsa_proj(x2d, pos1d, norm_w, w_in):
    blocks = (_blocks_cols(0, 2048, "rope", "qkv", 0) + _blocks_cols(2048, 512, "rope", "qkv", 2048)
              + _blocks_cols(2560, 512, "bf16", "qkv", 2560) + _blocks_cols(3072, 512, "rope", "qkv", 3072)
              + _blocks_cols(3584, 512, "bf16", "qkv", 3584) + _blocks_cols(4096, 512, "rope", "qkv", 4096)
              + _blocks_cols(4608, 512, "bf16", "qkv", 4608) + _blocks_cols(5120, 48, "sigmoid", "gate", 0)
              + _blocks_cols(5168, 2048, "silu", "sz", 0))
    outs = {"qkv": (5120, BF16), "gate": (48, F32), "sz": (2048, F32)}
    nc = _get("nsa_proj", lambda: build_proj(7216, blocks, "norm", outs, True))
    nw = np.ascontiguousarray(norm_w.reshape(16, 128).T)
    ident = np.eye(128, dtype=np.float32)
    invf = np.ascontiguousarray(np.broadcast_to(_inv_freq()[None, :], (128, 16)))
    in_maps = []
    for c in range(NCORE):
        sl = slice(c * TPC, (c + 1) * TPC)
        in_maps.append({"x": np.ascontiguousarray(x2d[sl]), "w": w_in, "nw": nw, "ident": ident,
                        "pos": np.ascontiguousarray(pos1d[sl].reshape(TPC // 128, 128).T), "invf": invf})
    res = run_bass_kernel_spmd(nc, in_maps, core_ids=list(range(NCORE)))
    qkv = np.concatenate([r["qkv"] for r in res.results], axis=0)
    gate = np.concatenate([r["gate"] for r in res.results], axis=0)
    sz = np.concatenate([r["sz"] for r in res.results], axis=0)
    return qkv, gate, sz


def _nsa_consts():
    k = np.arange(SEQ)
    E = (k[None, :] // 64 == np.arange(128)[:, None]).astype(np.float32).astype(NPBF)
    n = np.arange(512)
    blk = np.arange(128)
    ov = ((16 * n[:, None] < (blk[None, :] + 1) * 64) & (16 * n[:, None] + 32 > blk[None, :] * 64)).astype(np.float32)
    ov[511] = 0
    ov = np.ascontiguousarray(ov.reshape(4, 128, 128).transpose(1, 0, 2)).astype(NPBF)
    identb = np.eye(128, dtype=np.float32).astype(NPBF)
    return E, ov, identb


def nsa_core_inmaps(qkv, gate, sz, ck_pos, ck_w1, ck_w2, cv_pos, cv_w1, cv_w2):
    E, ov, identb = _nsa_consts()
    w1k = np.ascontiguousarray(ck_w1.transpose(1, 0, 2))
    w1v = np.ascontiguousarray(cv_w1.transpose(1, 0, 2))
    posk = np.ascontiguousarray(ck_pos.T)
    posv = np.ascontiguousarray(cv_pos.T)
    in_maps = []
    for c in range(NCORE):
        b, g = divmod(c, 4)
        rows = slice(b * SEQ, (b + 1) * SEQ)
        qk = qkv[rows]

        def colT(c0):
            return np.ascontiguousarray(qk[:, c0 + g * 128:c0 + (g + 1) * 128].T)

        def tokmaj(c0):
            return np.ascontiguousarray(qk[:, c0 + g * 128:c0 + (g + 1) * 128].reshape(64, 128, 128).transpose(1, 0, 2))

        qT = np.ascontiguousarray(qk[:, g * 512:(g + 1) * 512].reshape(SEQ, 4, 128).transpose(2, 1, 0))
        gt = gate[rows].reshape(SEQ, 3, 16)[:, :, 4 * g:4 * g + 4].reshape(SEQ, 12)
        gt = np.ascontiguousarray(gt.reshape(64, 128, 12).transpose(1, 0, 2))
        szz = np.ascontiguousarray(sz[rows][:, g * 512:(g + 1) * 512].reshape(64, 128, 512).transpose(1, 0, 2))
        in_maps.append({"qT": qT, "kcT": colT(2048), "vcT": colT(2560), "ksT": colT(3072), "kwT": colT(4096),
                        "vs": tokmaj(3584), "vw": tokmaj(4608), "gate": gt, "sz": szz,
                        "w1k": w1k, "w1v": w1v, "posk": posk, "posv": posv, "w2k": ck_w2, "w2v": cv_w2,
                        "E": E, "ov": ov, "identb": identb})
    return in_maps


def run_nsa_core(qkv, gate, sz, ck_pos, ck_w1, ck_w2, cv_pos, cv_w1, cv_w2, nqb=64):
    nc = _get(("nsa_core", nqb), lambda: build_nsa_core(nqb))
    in_maps = nsa_core_inmaps(qkv, gate, sz, ck_pos, ck_w1, ck_w2, cv_pos, cv_w1, cv_w2)
    res = run_bass_kernel_spmd(nc, in_maps, core_ids=list(range(NCORE)))
    o = np.zeros((NTOK, 2048), dtype=NPBF)
    for c in range(NCORE):
        b, g = divmod(c, 4)
        o[b * SEQ:(b + 1) * SEQ, g * 512:(g + 1) * 512] = res.results[c]["o"]
    return o


def run_hgrn_proj(x2d, norm_w, w_in):
    blocks = (_blocks_cols(0, 2048, "silu", "qs", 0) + _blocks_cols(2048, 2048, "f32", "fr", 0)
              + _blocks_cols(4096, 2048, "bf16", "iv", 0) + _blocks_cols(6144, 2048, "silu", "sz", 0))
    outs = {"qs": (2048, F32), "fr": (2048, F32), "iv": (2048, BF16), "sz": (2048, F32)}
    nc = _get("hgrn_proj", lambda: build_proj(8192, blocks, "norm", outs, False))
    nw = np.ascontiguousarray(norm_w.reshape(16, 128).T)
    ident = np.eye(128, dtype=np.float32)
    in_maps = [{"x": np.ascontiguousarray(x2d[c * TPC:(c + 1) * TPC]), "w": w_in, "nw": nw, "ident": ident} for c in range(NCORE)]
    res = run_bass_kernel_spmd(nc, in_maps, core_ids=list(range(NCORE)))
    return tuple(np.concatenate([r[k] for r in res.results], axis=0) for k in ("qs", "fr", "iv", "sz"))


def _hgrn_consts():
    s = np.arange(128)
    mbd = ((s[:, None] // 32 == s[None, :] // 32) & (s[:, None] <= s[None, :])).astype(np.float32)
    cm = (s[:, None] // 32 == np.arange(4)[None, :]).astype(np.float32)
    seg = np.ones((128, 2048), np.float32)
    seg[:, ::32] = 0.0
    ones = np.ones((128, 128), np.float32)
    identb = np.eye(128, dtype=np.float32).astype(NPBF)
    return mbd, cm, seg, ones, identb


def hgrn_core_inmaps(qs, fr, iv, sz, lb_logits, gnorm_w):
    mbd, cm, seg, ones, identb = _hgrn_consts()
    in_maps = []
    for c in range(NCORE):
        b, hg = divmod(c, 4)
        rows = slice(b * SEQ, (b + 1) * SEQ)
        cols = slice(hg * 512, (hg + 1) * 512)

        def featmaj(a):
            return np.ascontiguousarray(a[rows, cols].reshape(SEQ, 4, 128).transpose(2, 1, 0))

        ivv = np.ascontiguousarray(iv[rows, cols].reshape(64, 128, 4, 128).transpose(1, 0, 2, 3))
        lbl = np.ascontiguousarray(lb_logits[:, cols].reshape(2, 4, 128).transpose(2, 1, 0))
        in_maps.append({"qT": featmaj(qs), "fT": featmaj(fr), "szT": featmaj(sz), "iv": ivv, "lbl": lbl,
                        "gw": np.ascontiguousarray(gnorm_w.reshape(128, 1)), "maskbd": mbd, "cm": cm, "seg": seg,
                        "ones": ones, "identb": identb})
    return in_maps


def run_hgrn_core(qs, fr, iv, sz, lb_logits, gnorm_w, layer_j, nst=16):
    nc = _get(("hgrn_core", layer_j, nst), lambda: build_hgrn_core(layer_j, nst))
    in_maps = hgrn_core_inmaps(qs, fr, iv, sz, lb_logits, gnorm_w)
    res = run_bass_kernel_spmd(nc, in_maps, core_ids=list(range(NCORE)))
    o = np.zeros((NTOK, 2048), dtype=NPBF)
    for c in range(NCORE):
        b, hg = divmod(c, 4)
        o[b * SEQ:(b + 1) * SEQ, hg * 512:(hg + 1) * 512] = res.results[c]["oT"].transpose(2, 1, 0).reshape(SEQ, 512)
    return o


def run_outproj(o_bf16, x2d, w_out):
    blocks = _blocks_cols(0, 2048, "resid", "y", 0)
    nc = _get("outproj", lambda: build_proj(2048, blocks, "lhsT", {"y": (2048, F32)}, False))
    in_maps = []
    for c in range(NCORE):
        sl = slice(c * TPC, (c + 1) * TPC)
        oT = np.ascontiguousarray(o_bf16[sl].T.reshape(16, 128, TPC).transpose(1, 0, 2))
        in_maps.append({"x": np.ascontiguousarray(x2d[sl]), "w": w_out, "aT": oT})
    res = run_bass_kernel_spmd(nc, in_maps, core_ids=list(range(NCORE)))
    return np.concatenate([r["y"] for r in res.results], axis=0)


def build_final_norm():
    nc = bass.Bass("TRN2", target_bir_lowering=False)
    P = Prog(nc)
    NT = TPC // 128
    x = nc.dram_tensor("x", [TPC, D_MODEL], F32, kind="ExternalInput").ap()
    fwd = nc.dram_tensor("fw", [128, D_MODEL], F32, kind="ExternalInput").ap()
    y = nc.dram_tensor("y", [TPC, D_MODEL], F32, kind="ExternalOutput").ap()
    fw = P.sbuf([128, D_MODEL], F32, "fw")
    d_fw = P.dma("act", fw[:], fwd)
    ss = P.sbuf([128, NT], F32, "ss")
    rstd = P.sbuf([128, NT], F32, "rstd")
    junk = P.sbuf([128, D_MODEL], BF16, "junk")
    xr = Ring([P.sbuf([128, D_MODEL], F32, f"xt{i}") for i in range(3)])
    yr = Ring([P.sbuf([128, D_MODEL], F32, f"yt{i}") for i in range(3)])
    outs = []
    for t in range(NT):
        k, xt, rd = xr.next()
        d_x = P.dma("sp", xt[:], x[t * 128:(t + 1) * 128, :], deps=rd)
        a_sq = P.op("act", lambda e, xt=xt, t=t: e.activation(out=junk[:], in_=xt[:], func=AF.Square, accum_out=ss[:, t:t + 1]), deps=[d_x])
        v1 = P.op("dve", lambda e, t=t: e.tensor_scalar(out=rstd[:, t:t + 1], in0=ss[:, t:t + 1], scalar1=1.0 / D_MODEL, scalar2=RMS_EPS,
                                                        op0=ALU.mult, op1=ALU.add), deps=[a_sq])
        a_rt = P.op("act", lambda e, t=t: e.activation(out=rstd[:, t:t + 1], in_=rstd[:, t:t + 1], func=AF.Sqrt), deps=[v1])
        v2 = P.op("dve", lambda e, t=t: e.reciprocal(out=rstd[:, t:t + 1], in_=rstd[:, t:t + 1]), deps=[a_rt])
        ky, yt, rdy = yr.next()
        v3 = P.op("dve", lambda e, yt=yt, xt=xt, t=t: e.scalar_tensor_tensor(out=yt[:], in0=xt[:], scalar=rstd[:, t:t + 1], in1=fw[:],
                                                                             op0=ALU.mult, op1=ALU.mult), deps=[v2, d_x, d_fw] + rdy)
        xr.read_by(k, a_sq, v3)
        od = P.dma("pool", y[t * 128:(t + 1) * 128, :], yt[:], deps=[v3])
        yr.read_by(ky, od)
        outs.append(od)
    P.wait_all("pool", outs)
    P.emit()
    return nc


def run_final_norm(x2d, fw):
    nc = _get("final_norm", build_final_norm)
    fwb = np.ascontiguousarray(np.broadcast_to(fw[None, :], (128, D_MODEL)))
    in_maps = [{"x": np.ascontiguousarray(x2d[c * TPC:(c + 1) * TPC]), "fw": fwb} for c in range(NCORE)]
    res = run_bass_kernel_spmd(nc, in_maps, core_ids=list(range(NCORE)))
    return np.concatenate([r["y"] for r in res.results], axis=0)


def kernel(x, positions, norm_w, final_norm_w, nsa_w_in, nsa_ck_pos, nsa_ck_w1, nsa_ck_w2,
           nsa_cv_pos, nsa_cv_w1, nsa_cv_w2, nsa_w_out, hgrn_w_in, hgrn_lb_logits, hgrn_gnorm_w, hgrn_w_out):
    f = lambda a: np.ascontiguousarray(np.asarray(a, dtype=np.float32))
    x2d = f(x).reshape(NTOK, D_MODEL)
    pos1d = np.ascontiguousarray(np.asarray(positions).astype(np.int32).reshape(-1))
    norm_w = f(norm_w)
    for layer in range(4):
        j = layer // 2
        if layer % 2 == 0:
            qkv, gate, sz = run_nsa_proj(x2d, pos1d, norm_w[layer], f(nsa_w_in[j]))
            o = run_nsa_core(qkv, gate, sz, f(nsa_ck_pos[j]), f(nsa_ck_w1[j]), f(nsa_ck_w2[j]),
                             f(nsa_cv_pos[j]), f(nsa_cv_w1[j]), f(nsa_cv_w2[j]))
            x2d = run_outproj(o, x2d, f(nsa_w_out[j]))
        else:
            qs, fr, iv, sz = run_hgrn_proj(x2d, norm_w[layer], f(hgrn_w_in[j]))
            o = run_hgrn_core(qs, fr, iv, sz, f(hgrn_lb_logits), f(hgrn_gnorm_w[j]), j)
            x2d = run_outproj(o, x2d, f(hgrn_w_out[j]))
    out = run_final_norm(x2d, f(final_norm_w))
    return out.reshape(BATCH, SEQ, D_MODEL)
```
